# Optimizing a Trainium2 kernel written in Bass

```python
import jax, jax.numpy as jnp
from jax import lax
import numpy as np

D_MODEL = 1024
BATCH = 8
SEQ = 2048
DEPTH = 4
DEC_BATCH = 128
DEC_SEQ = 1
PAST_LEN = 16384
PAGE_SIZE = 128

N_MIXERS = 2
N_POOL_LAYERS = (DEPTH + N_MIXERS - 1) // N_MIXERS
N_GLA_LAYERS = DEPTH // N_MIXERS

POOL_WINDOWS = (2, 4, 8, 16)
POOL_GROUPS = len(POOL_WINDOWS)
POOL_GC = D_MODEL // POOL_GROUPS
POOL_BUF = max(POOL_WINDOWS) - 1

GLA_HEADS = 4
GLA_DK = D_MODEL // 2
GLA_DV = D_MODEL
HEAD_K = GLA_DK // GLA_HEADS
HEAD_V = GLA_DV // GLA_HEADS
GATE_RANK = 16
GATE_TAU = 16.0
GLA_CHUNK = 64
GLA_IN = 2 * GLA_DK + 2 * GLA_DV + GATE_RANK

D_FF = -(-8 * D_MODEL // (3 * 256)) * 256

EPS = 1e-6

kernel_name = 'hybrid_pool_gla_adaln_decoder'


def _rmsnorm(x, g):
    xf = x.astype(jnp.float32)
    y = xf * lax.rsqrt(jnp.mean(xf * xf, axis=-1, keepdims=True) + EPS)
    return (y * g.astype(jnp.float32)).astype(x.dtype)


def _modulate(x, g, shift, scale):
    h = _rmsnorm(x, g)
    return h * (1 + scale[:, None, :]) + shift[:, None, :]


def _ada(c, w, b):
    m = jax.nn.silu(c) @ w + b
    return jnp.split(m, 6, axis=-1)


def _pool_mix(h, buf, pos0, w_pool, s_pool):
    B, T, D = h.shape
    z = jnp.concatenate([buf.astype(h.dtype), h], axis=1)
    zf = z.astype(jnp.float32)
    cs = jnp.concatenate([jnp.zeros((B, 1, D), jnp.float32), jnp.cumsum(zf, axis=1)], axis=1)
    end = cs[:, POOL_BUF + 1:]
    n_avail = pos0 + jnp.arange(T) + 1
    means = []
    for g, w in enumerate(POOL_WINDOWS):
        sl = slice(g * POOL_GC, (g + 1) * POOL_GC)
        start = cs[:, POOL_BUF + 1 - w: POOL_BUF + 1 - w + T, sl]
        cnt = jnp.minimum(n_avail, w).astype(jnp.float32)[None, :, None]
        means.append((end[:, :, sl] - start) / cnt)
    d = jnp.concatenate(means, axis=-1) - zf[:, POOL_BUF:]
    d = d.astype(h.dtype).reshape(B, T, POOL_GROUPS, POOL_GC)
    y = jnp.einsum('btgc,gcd->btgd', d, w_pool).reshape(B, T, D) * s_pool
    return y, z[:, -POOL_BUF:]


def _gla_scan(q, k, v, log_a, S0):
    B, H, T, _ = q.shape
    C = min(GLA_CHUNK, T)
    n = -(-T // C)
    pad = n * C - T

    def blocks(t):
        t = jnp.pad(t, ((0, 0), (0, 0), (0, pad), (0, 0)))
        return jnp.moveaxis(t.reshape(B, H, n, C, t.shape[-1]), 2, 0)

    mask = jnp.tril(jnp.ones((C, C), dtype=bool))[:, :, None]

    def step(S, inp):
        qc, kc, vc, ac = inp
        b = jnp.cumsum(ac, axis=2)
        diff = b[:, :, :, None, :] - b[:, :, None, :, :]
        decay = jnp.exp(jnp.where(mask, diff, -jnp.inf))
        A = jnp.einsum('bhtd,bhsd,bhtsd->bhts', qc, kc, decay)
        o = jnp.einsum('bhts,bhsv->bhtv', A, vc) + jnp.einsum('bhtd,bhdv->bhtv', qc * jnp.exp(b), S)
        b_last = b[:, :, -1:, :]
        S = jnp.exp(b_last[:, :, 0, :])[..., None] * S + jnp.einsum('bhsd,bhsv->bhdv', kc * jnp.exp(b_last - b), vc)
        return S, o

    S, o = lax.scan(step, S0, (blocks(q), blocks(k), blocks(v), blocks(log_a)))
    o = jnp.moveaxis(o, 0, 2).reshape(B, H, n * C, -1)[:, :, :T]
    return o, S


def _gla_mix(h, S0, w_in, w_g2, b_g, g_norm, w_o):
    B, T, _ = h.shape
    proj = h @ w_in
    q, k, v, r, gl = jnp.split(proj, [GLA_DK, 2 * GLA_DK, 2 * GLA_DK + GLA_DV, 2 * GLA_DK + 2 * GLA_DV], axis=-1)
    log_a = jax.nn.log_sigmoid((gl @ w_g2 + b_g).astype(jnp.float32)) / GATE_TAU

    def heads(t, d):
        return t.reshape(B, T, GLA_HEADS, d).transpose(0, 2, 1, 3).astype(jnp.float32)

    o, S = _gla_scan(heads(q, HEAD_K) * (HEAD_K ** -0.5), heads(k, HEAD_K), heads(v, HEAD_V),
                     heads(log_a, HEAD_K), S0.astype(jnp.float32))
    o = _rmsnorm(o.transpose(0, 2, 1, 3), g_norm)
    o = o.reshape(B, T, GLA_DV) * jax.nn.silu(r.astype(jnp.float32))
    return o.astype(h.dtype) @ w_o, S.astype(S0.dtype)


def _swiglu(h, w_gate, w_up, w_down):
    return (jax.nn.silu(h @ w_gate) * (h @ w_up)) @ w_down


def _trunk(x, c, pool_buf, gla_S, pos0, g_mix, g_ffn, w_ada, b_ada, w_pool, s_pool,
           w_gla_in, w_gla_g2, b_gla_g, g_gla_norm, w_gla_o, w_ffn_gate, w_ffn_up,
           w_ffn_down, g_final):
    new_pool, new_gla = [], []
    for l in range(DEPTH):
        sh_m, sc_m, gt_m, sh_f, sc_f, gt_f = _ada(c, w_ada[l], b_ada[l])
        h = _modulate(x, g_mix[l], sh_m, sc_m)
        i = l // N_MIXERS
        if l % N_MIXERS == 0:
            y, buf = _pool_mix(h, pool_buf[i], pos0, w_pool[i], s_pool[i])
            new_pool.append(buf)
        else:
            y, S = _gla_mix(h, gla_S[i], w_gla_in[i], w_gla_g2[i], b_gla_g[i], g_gla_norm[i], w_gla_o[i])
            new_gla.append(S)
        x = x + gt_m[:, None, :] * y
        h = _modulate(x, g_ffn[l], sh_f, sc_f)
        x = x + gt_f[:, None, :] * _swiglu(h, w_ffn_gate[l], w_ffn_up[l], w_ffn_down[l])
    return _rmsnorm(x, g_final), jnp.stack(new_pool), jnp.stack(new_gla)


def setup_inputs(seed: int = 0) -> dict:
    key = jax.random.key(seed)
    ks = jax.random.split(key, 24)
    f32 = jnp.float32
    nrm = lambda k, s, sc=1.0: jax.random.normal(k, s, f32) * sc
    return {
        'x_prompt': nrm(ks[0], (BATCH, SEQ, D_MODEL)),
        'x_sample': nrm(ks[1], (DEC_BATCH, DEC_SEQ, D_MODEL)),
        'c_prompt': nrm(ks[2], (BATCH, D_MODEL)),
        'c_sample': nrm(ks[3], (DEC_BATCH, D_MODEL)),
        'state_pool': nrm(ks[4], (N_POOL_LAYERS, DEC_BATCH, POOL_BUF, D_MODEL)),
        'state_gla': nrm(ks[5], (N_GLA_LAYERS, DEC_BATCH, GLA_HEADS, HEAD_K, HEAD_V)),
        'g_mix': 1.0 + nrm(ks[6], (DEPTH, D_MODEL), 0.02),
        'g_ffn': 1.0 + nrm(ks[7], (DEPTH, D_MODEL), 0.02),
        'w_ada': nrm(ks[8], (DEPTH, D_MODEL, 6 * D_MODEL), 0.5 * D_MODEL ** -0.5),
        'b_ada': nrm(ks[9], (DEPTH, 6 * D_MODEL), 0.02),
        'w_pool': nrm(ks[10], (N_POOL_LAYERS, POOL_GROUPS, POOL_GC, POOL_GC), POOL_GC ** -0.5),
        's_pool': 1.0 + nrm(ks[11], (N_POOL_LAYERS, D_MODEL), 0.1),
        'w_gla_in': nrm(ks[12], (N_GLA_LAYERS, D_MODEL, GLA_IN), D_MODEL ** -0.5),
        'w_gla_g2': nrm(ks[13], (N_GLA_LAYERS, GATE_RANK, GLA_DK), GATE_RANK ** -0.5),
        'b_gla_g': nrm(ks[14], (N_GLA_LAYERS, GLA_DK), 0.1),
        'g_gla_norm': 1.0 + nrm(ks[15], (N_GLA_LAYERS, HEAD_V), 0.02),
        'w_gla_o': nrm(ks[16], (N_GLA_LAYERS, GLA_DV, D_MODEL), GLA_DV ** -0.5),
        'w_ffn_gate': nrm(ks[17], (DEPTH, D_MODEL, D_FF), D_MODEL ** -0.5),
        'w_ffn_up': nrm(ks[18], (DEPTH, D_MODEL, D_FF), D_MODEL ** -0.5),
        'w_ffn_down': nrm(ks[19], (DEPTH, D_FF, D_MODEL), D_FF ** -0.5),
        'g_final': 1.0 + nrm(ks[20], (D_MODEL,), 0.02),
    }


def reference(x_prompt, x_sample, c_prompt, c_sample, state_pool, state_gla, g_mix, g_ffn,
              w_ada, b_ada, w_pool, s_pool, w_gla_in, w_gla_g2, b_gla_g, g_gla_norm, w_gla_o,
              w_ffn_gate, w_ffn_up, w_ffn_down, g_final):
    weights = (g_mix, g_ffn, w_ada, b_ada, w_pool, s_pool, w_gla_in, w_gla_g2, b_gla_g,
               g_gla_norm, w_gla_o, w_ffn_gate, w_ffn_up, w_ffn_down, g_final)
    pool0 = jnp.zeros((N_POOL_LAYERS, x_prompt.shape[0], POOL_BUF, D_MODEL), x_prompt.dtype)
    gla0 = jnp.zeros((N_GLA_LAYERS, x_prompt.shape[0], GLA_HEADS, HEAD_K, HEAD_V), jnp.float32)
    y_prompt, pool_p, gla_p = _trunk(x_prompt, c_prompt, pool0, gla0, 0, *weights)
    y_sample, pool_s, gla_s = _trunk(x_sample, c_sample, state_pool, state_gla, PAST_LEN, *weights)
    return (y_prompt, y_sample, pool_p, pool_s, gla_p, gla_s)
```

```python
import numpy as np
import concourse.bass as bass
import concourse.mybir as mybir
from concourse.bass_utils import run_bass_kernel_spmd
from contextlib import ExitStack

F32 = mybir.dt.float32
BF16 = mybir.dt.bfloat16
ALU = mybir.AluOpType
AF = mybir.ActivationFunctionType
AX = mybir.AxisListType
SAME_ENG_GAP = 2


class Buf:
    __slots__ = ("name", "t", "lw", "rd", "sem", "semval", "ssem", "ssemval")

    def __init__(self, name, t=None):
        self.name = name
        self.t = t
        self.lw = None
        self.rd = {}
        self.sem = None
        self.semval = 0
        self.ssem = None
        self.ssemval = 0


class Prog:
    ENG = ("pe", "act", "dve", "pool", "sp")

    def __init__(self, nc, stack):
        self.nc = nc
        self.stack = stack
        self.ops = {e: [] for e in self.ENG}
        self.cnt = {e: 0 for e in self.ENG}
        self.seen = {e: {} for e in self.ENG}
        self.esem = {e: stack.enter_context(nc.semaphore("es_" + e)) for e in self.ENG}
        self.same = {"pe": False, "act": True, "dve": True, "pool": True, "sp": False}
        self.final = []
        self.nsem = 5
        self.sempool = {}

    def sb(self, name, shape, dtype):
        t = self.stack.enter_context(self.nc.sbuf_tensor(name, list(shape), dtype))
        return Buf(name, t)

    def ps(self, name, shape=(128, 512), dtype=F32):
        t = self.stack.enter_context(self.nc.psum_tensor(name, list(shape), dtype))
        return Buf(name, t)

    def newsem(self, name):
        self.nsem += 1
        return self.stack.enter_context(self.nc.semaphore(name))

    def _waits(self, eng, reads, writes, skip_dma_waw=None):
        own = self.esem[eng]
        deps = []
        for b in reads:
            if b.lw is not None:
                if b.lw[0] is own and (not self.same[eng] or (SAME_ENG_GAP is not None and self.cnt[eng] - b.lw[1] >= SAME_ENG_GAP)):
                    continue
                deps.append(b.lw)
        for b in writes:
            if b.lw is not None and not (skip_dma_waw is not None and b.lw[0] is skip_dma_waw):
                if b.lw[0] is not own or (SAME_ENG_GAP is None and self.same[eng]):
                    deps.append(b.lw)
            for d in b.rd.values():
                if d[0] is not own or (SAME_ENG_GAP is None and self.same[eng]):
                    deps.append(d)
        seen = self.seen[eng]
        best = {}
        for sem, val in deps:
            k = id(sem)
            if seen.get(k, 0) >= val:
                continue
            if k not in best or best[k][1] < val:
                best[k] = (sem, val)
        for k, (sem, val) in best.items():
            seen[k] = val
        return list(best.values())

    def op(self, eng, fn, reads=(), writes=()):
        waits = self._waits(eng, reads, writes)
        self.cnt[eng] += 1
        tick = (self.esem[eng], self.cnt[eng])
        self.ops[eng].append((waits, fn, (self.esem[eng], 1)))
        for b in reads:
            b.rd[id(tick[0])] = tick
        for b in writes:
            b.lw = tick
            b.rd = {}
        return tick

    def dma(self, q, out_ap, in_ap, reads=(), writes=(), out=False, **kw):
        if writes:
            key = "ld_" + writes[0].name
        elif reads:
            key = "st_" + reads[0].name
        else:
            key = "outsem"
        if key not in self.sempool:
            self.sempool[key] = [self.newsem(key), 0]
        ent = self.sempool[key]
        sem = ent[0]
        ent[1] += 16
        val = ent[1]
        waits = self._waits(q, reads, writes, skip_dma_waw=sem)
        tok = (sem, val)

        def fn(e, out_ap=out_ap, in_ap=in_ap, kw=kw):
            return e.dma_start(out=out_ap, in_=in_ap, **kw)

        self.ops[q].append((waits, fn, (sem, 16)))
        for b in reads:
            b.rd[id(sem)] = tok
        for b in writes:
            b.lw = tok
            b.rd = {}
        if out:
            for i, (s, v) in enumerate(self.final):
                if s is sem:
                    self.final[i] = tok
                    break
            else:
                self.final.append(tok)
        return tok

    def barrier(self, extra=()):
        toks = [(self.esem[e], self.cnt[e]) for e in self.ENG if self.cnt[e] > 0] + list(extra)
        for e in self.ENG:
            waits = []
            for sem, val in toks:
                if sem is self.esem[e]:
                    continue
                if self.seen[e].get(id(sem), 0) >= val:
                    continue
                self.seen[e][id(sem)] = val
                waits.append((sem, val))
            if waits:
                self.ops[e].append((waits, None, None))

    def phase_end(self, extra=()):
        toks = [(self.esem[e], self.cnt[e]) for e in self.ENG if self.cnt[e] > 0]
        for e in self.ENG:
            waits = []
            for sem, val in (toks if e == "sp" else []) + list(extra):
                if sem is self.esem[e]:
                    continue
                if self.seen[e].get(id(sem), 0) >= val:
                    continue
                self.seen[e][id(sem)] = val
                waits.append((sem, val))
            if waits:
                self.ops[e].append((waits, None, None))

    def finish(self):
        waits = list(self.final)
        for e in self.ENG:
            if e != "sp" and self.cnt[e] > 0:
                waits.append((self.esem[e], self.cnt[e]))
        self.ops["sp"].append((waits, None, None))
        nc = self.nc
        ops = self.ops

        def run(name, e):
            for waits, fn, inc in ops[name]:
                for sem, val in waits:
                    e.wait_ge(sem, val)
                if fn is not None:
                    ins = fn(e)
                    ins.then_inc(inc[0], inc[1])

        with nc.Block() as block:
            @block.tensor
            def _(e):
                run("pe", e)

            @block.scalar
            def _(e):
                run("act", e)

            @block.vector
            def _(e):
                run("dve", e)

            @block.gpsimd
            def _(e):
                run("pool", e)

            @block.sync
            def _(e):
                run("sp", e)


P = 128
D = 1024
KC = 8
SEQ = 2048
NTOK = 1024
NS = 16
NCOL = NTOK + NS
DEPTH = 4
FF = 2816
NJ = FF // P
HK = 128
HV = 256
NH = 4
GR = 16
EPS = 1e-6
NV = 35
RING = 6
PREP_LEAD = 3
SCRW = 25500
WIN = (2, 4, 8, 16)

N_ADA = 24
N_POOL = 1
N_QK, N_VR, N_WO = 4, 8, 4
N_GU, N_DN = 22, 11


def _pieces_from_blocks(blk):
    nb = blk.shape[0]
    assert nb % 16 == 0
    return np.ascontiguousarray(blk.reshape(nb // 16, 16, P, P).transpose(0, 2, 1, 3)).reshape(nb // 16, P, 16 * P)


def _oc_major(W):
    K, N = W.shape
    return W.reshape(K // P, P, N // P, P).transpose(2, 0, 1, 3).reshape(-1, P, P)


def build_wstream(w_ada, w_pool, w_gla_in, w_gla_o, w_ffn_gate, w_ffn_up, w_ffn_down):
    pcs = []
    index = {}
    pos = 0

    def add(key, arr):
        nonlocal pos
        index[key] = (pos, arr.shape[0])
        pcs.append(arr)
        pos += arr.shape[0]

    for l in range(DEPTH):
        add(("ada", l), _pieces_from_blocks(_oc_major(w_ada[l])))
    for l in range(DEPTH):
        i = l // 2
        if l % 2 == 0:
            wp = w_pool[i]
            blk = wp.reshape(4, 2, P, 2, P).transpose(0, 3, 1, 2, 4).reshape(16, P, P)
            add(("pool", l), _pieces_from_blocks(blk))
        else:
            W = w_gla_in[i]
            add(("qk", l), _pieces_from_blocks(_oc_major(W[:, 0:1024])))
            Wvr = W[:, 1024:3072]
            blk = Wvr.reshape(KC, P, 4, 4, P).transpose(2, 0, 3, 1, 4).reshape(128, P, P)
            add(("vr", l), _pieces_from_blocks(blk))
            add(("wo", l), _pieces_from_blocks(_oc_major(w_gla_o[i])))
        g = w_ffn_gate[l].reshape(KC, P, NJ, P).transpose(2, 0, 1, 3)
        u = w_ffn_up[l].reshape(KC, P, NJ, P).transpose(2, 0, 1, 3)
        blk = np.concatenate([g, u], axis=1).reshape(NJ * 16, P, P)
        add(("gu", l), _pieces_from_blocks(blk))
        dn = w_ffn_down[l].reshape(NJ, P, KC, P).transpose(2, 0, 1, 3).reshape(KC * NJ, P, P)
        add(("dn", l), _pieces_from_blocks(dn))
    return np.concatenate(pcs, axis=0), index


def wstream_index():
    index = {}
    pos = 0
    for l in range(DEPTH):
        index[("ada", l)] = (pos, N_ADA)
        pos += N_ADA
    for l in range(DEPTH):
        if l % 2 == 0:
            index[("pool", l)] = (pos, N_POOL); pos += N_POOL
        else:
            index[("qk", l)] = (pos, N_QK); pos += N_QK
            index[("vr", l)] = (pos, N_VR); pos += N_VR
            index[("wo", l)] = (pos, N_WO); pos += N_WO
        index[("gu", l)] = (pos, N_GU); pos += N_GU
        index[("dn", l)] = (pos, N_DN); pos += N_DN
    return index, pos


class Scratch:
    def __init__(self, t, nwords):
        self.t = t
        self.n = nwords
        self.off = 0

    def reset(self, off=0):
        self.off = off

    def alloc(self, name, shape, dtype):
        nel = 1
        for s in shape[1:]:
            nel *= s
        words = nel if dtype == F32 else (nel + 1) // 2
        ap = self.t[:, self.off:self.off + words]
        self.off += words
        assert self.off <= self.n, (name, self.off, self.n)
        if dtype == BF16:
            ap = ap.bitcast(BF16)
            if nel % 2:
                ap = ap[:, 0:nel]
        if len(shape) == 3:
            ap = ap.rearrange("p (a b) -> p a b", a=shape[1])
        elif len(shape) == 4:
            ap = ap.rearrange("p (a b c) -> p a b c", a=shape[1], b=shape[2])
        if shape[0] < P:
            ap = ap[0:shape[0]]
        return Buf(name, ap)


def _tt(p, eng, out, in0, in1, op, R, W):
    p.op(eng, lambda e: e.tensor_tensor(out=out, in0=in0, in1=in1, op=op), reads=R, writes=W)


def _stt(p, eng, out, in0, scalar, in1, op0, op1, R, W):
    p.op(eng, lambda e: e.scalar_tensor_tensor(out=out, in0=in0, scalar=scalar, in1=in1, op0=op0, op1=op1),
         reads=R, writes=W)


def _ts(p, eng, out, in0, s1, s2, op0, op1, R, W):
    if s2 is None:
        p.op(eng, lambda e: e.tensor_scalar(out=out, in0=in0, scalar1=s1, scalar2=None, op0=op0), reads=R, writes=W)
    else:
        p.op(eng, lambda e: e.tensor_scalar(out=out, in0=in0, scalar1=s1, scalar2=s2, op0=op0, op1=op1),
             reads=R, writes=W)


def _act(p, out, in_, func, R, W, bias=None, scale=None, accum=None):
    kw = {}
    if bias is not None:
        kw["bias"] = bias
    if scale is not None:
        kw["scale"] = scale
    if accum is not None:
        kw["accum_out"] = accum
    p.op("act", lambda e: e.activation(out=out, in_=in_, func=func, **kw), reads=R, writes=W)


def _cp(p, eng, out, in_, R, W):
    if eng == "act":
        p.op("act", lambda e: e.activation(out=out, in_=in_, func=AF.Copy), reads=R, writes=W)
    else:
        p.op(eng, lambda e: e.tensor_copy(out=out, in_=in_), reads=R, writes=W)


def _mm(p, out, pairs, R, W, start=True, stop=True):
    pairs = list(pairs)

    def fn(e):
        ins = None
        n = len(pairs)
        for i, (l, r) in enumerate(pairs):
            ins = e.matmul(out, l, r, start=(start and i == 0), stop=(stop and i == n - 1))
        return ins

    p.op("pe", fn, reads=R, writes=W)


def _tr(p, out, in_, ident, R, W):
    p.op("pe", lambda e: e.transpose(out, in_, ident), reads=R, writes=W)


class Banks:
    def __init__(self, p, n=8):
        self.free = [p.ps("pb%d" % i) for i in range(n)]

    def get(self):
        return self.free.pop(0)

    def put(self, b):
        self.free.append(b)


class Stream:
    def __init__(self, p, wst, ring):
        self.p = p
        self.wst = wst
        self.ring = ring
        self.seq = 0

    def next(self, idx):
        r = self.ring[self.seq % len(self.ring)]
        self.seq += 1
        self.p.dma("pool", r.t[:, :], self.wst[idx], writes=[r])
        return r


def blk(r, b, n=1):
    return r.t[:, b * P:(b + n) * P]


def build_program():
    nc = bass.Bass("TRN2", target_bir_lowering=False)
    widx, NPIECE = wstream_index()

    def din(name, shape):
        return nc.dram_tensor(name, list(shape), F32, kind="ExternalInput").ap()

    def dout(name, shape):
        return nc.dram_tensor(name, list(shape), F32, kind="ExternalOutput").ap()

    xpT = din("xpT", [D, SEQ])
    xsT = din("xsT", [D, NS])
    cT = din("cT", [D, 17])
    spT = din("spT", [2, D, NS * 15])
    sg = din("sg", [2, NS, NH, HK, HV])
    wst = din("wst", [NPIECE, P, 2048])
    pvec = din("pvec", [D, NV])
    w2a = din("w2a", [2, 17, 512])
    wgl = din("wgl", [2, P, KC * GR])
    gnr = din("gnr", [P, 2 * HV])
    cst_f = din("cst_f", [P, 128 + 256 + 64])
    cst_b = din("cst_b", [P, 384])
    sel = din("sel", [NS, NS * P])
    ypT = dout("ypT", [D, SEQ])
    ysT = dout("ysT", [D, NS])
    ppT = dout("ppT", [2, D, 15])
    psT = dout("psT", [2, D, NS * 15])
    gp = dout("gp", [2, NH, HK, HV])
    gs = dout("gs", [2, NS, NH, HK, HV])

    with ExitStack() as stack:
        p = Prog(nc, stack)
        BK = Banks(p)
        XT = p.sb("XT", [P, KC, NCOL], F32)
        HT = p.sb("HT", [P, KC, NCOL], BF16)
        XB = [[Buf("x%d_%d" % (k, s)) for s in range(3)] for k in range(KC)]
        HB = [[Buf("h%d_%d" % (k, s)) for s in range(3)] for k in range(KC)]
        ring = [p.sb("ring%d" % i, [P, 2048], BF16) for i in range(RING)]
        ST = Stream(p, wst, ring)
        SST = p.sb("SST", [P, 2, NH, HV], F32)
        SSTB = [Buf("sst0"), Buf("sst1")]
        M = p.sb("M", [P, DEPTH, 48, 17], F32)
        Ml = [Buf("M%d" % l) for l in range(DEPTH)]
        Mf = [Buf("Mf%d" % l) for l in range(DEPTH)]

        def MB(l, idx):
            return Ml[l] if idx < 3 else Mf[l]

        HP = p.sb("HP", [P, 2, KC, 15], F32)
        HPB = [Buf("hp0"), Buf("hp1")]
        PV = p.sb("PV", [P, KC, NV], F32)
        CF = p.sb("CF", [P, 448], F32)
        CB = p.sb("CB", [P, 384], BF16)
        SEL = p.sb("SEL", [NS, NS * P], BF16)
        W2A = p.sb("W2A", [17, 2, 512], BF16)
        WGL = p.sb("WGL", [P, 2, KC * GR], BF16)
        GNR = p.sb("GNR", [P, 2 * HV], F32)
        SCB = p.sb("SCB", [P, KC, 17], BF16)
        SCRT = p.sb("SCR", [P, SCRW], F32)
        scr = Scratch(SCRT.t, SCRW)
        U = CF.t[:, 0:128]
        I16B = CF.t[:, 128:384].rearrange("p (a b) -> p a b", a=16)
        INVC = CF.t[:, 384:448].rearrange("p (g t) -> p g t", g=4)
        ONES = CB.t[:, 0:128]
        IDB = CB.t[:, 128:256]
        UB = CB.t[:, 256:384]
        DEPS = float(D * EPS)

        SQ = scr.alloc("SQ", [P, 4, 512], BF16)
        T2 = scr.alloc("T2", [P, 2, 512], F32)
        T2b = [Buf("t2a"), Buf("t2b")]
        RSTD = scr.alloc("RSTD", [P, 512], F32)
        SCR0 = scr.off

        p.dma("sp", PV.t[:, :, :], pvec.rearrange("(k p) n -> p k n", p=P), writes=[PV])
        p.dma("sp", CF.t[:, :], cst_f, writes=[CF])
        p.dma("pool", CB.t[:, :], cst_b, writes=[CB])
        p.dma("pool", SEL.t[:, :], sel, writes=[SEL])
        p.dma("pool", W2A.t[:, :, :], w2a.rearrange("i r c -> r i c"), writes=[W2A])
        p.dma("pool", WGL.t[:, :, :], wgl.rearrange("i p n -> p i n"), writes=[WGL])
        p.dma("sp", GNR.t[:, :], gnr, writes=[GNR])
        _ts(p, "dve", PV.t[:, :, 0:8], PV.t[:, :, 0:8], 32.0, None, ALU.mult, None, [PV], [PV])
        _ts(p, "dve", PV.t[:, :, 34:35], PV.t[:, :, 34:35], 32.0, None, ALU.mult, None, [PV], [PV])

        scr.reset(SCR0)
        SCF = scr.alloc("SCF", [P, KC, 17], F32)
        p.dma("sp", SCF.t[:, :, :], cT.rearrange("(k p) r -> p k r", p=P), writes=[SCF])
        _act(p, SCB.t[:, :, :], SCF.t[:, :, :], AF.Silu, [SCF], [SCB])

        def ada_half(l, hf):
            base, n = widx[("ada", l)]
            bk = BK.get()
            mb = MB(l, 3 * hf)
            for pc in range(hf * 12, hf * 12 + 12):
                r = ST.next(base + pc)
                for o2 in range(2):
                    oc = pc * 2 + o2
                    c0 = (oc - hf * 24) * 17
                    _mm(p, bk.t[:, c0:c0 + 17], [(blk(r, o2 * 8 + kc), SCB.t[:, kc, :]) for kc in range(KC)],
                        [r, SCB], [bk])
                yield
            _cp(p, "dve", M.t[:, l, hf * 24:hf * 24 + 24, :], bk.t[:, 0:408].rearrange("p (a b) -> p a b", b=17), [bk], [mb])
            BK.put(bk)
            Mv = M.t[:, l, hf * 24:hf * 24 + 24, :].rearrange("p (i k) r -> p i k r", i=3)
            c0 = 8 + l * 6 + hf * 3
            bv = PV.t[:, :, c0:c0 + 3].rearrange("p k i -> p i k").unsqueeze(3).broadcast_to([P, 3, KC, 17])
            _tt(p, "dve", Mv, Mv, bv, ALU.add, [mb, PV], [mb])
            idx, col = (1, l) if hf == 0 else (4, 4 + l)
            mv = M.t[:, l, idx * 8:(idx + 1) * 8, :]
            gv = PV.t[:, :, col:col + 1].broadcast_to([P, KC, 17])
            _stt(p, "dve", mv, mv, 1.0, gv, ALU.add, ALU.mult, [mb, PV], [mb])
            if hf == 0 and l % 2 == 0:
                mv = M.t[:, l, 16:24, :]
                sv = PV.t[:, :, 32 + l // 2:33 + l // 2].broadcast_to([P, KC, 17])
                _tt(p, "dve", mv, mv, sv, ALU.mult, [mb, PV], [mb])
            yield

        def ada_layer(l):
            for hf in range(2):
                for _ in ada_half(l, hf):
                    yield

        for _ in ada_half(0, 0):
            pass
        p.barrier()

        SQ2 = T2.t[:, :, :].rearrange("p a b -> p (a b)").bitcast(BF16).rearrange("p (k t) -> p k t", k=4)

        def norm_stats(st, o, w):
            bk = BK.get()
            _act(p, SQ.t[:, :, 0:w], XT.t[:, 0:4, o:o + w], AF.Square, [XB[k][st] for k in range(4)], [SQ])
            _tt(p, "dve", SQ2[:, :, 0:w], XT.t[:, 4:8, o:o + w], XT.t[:, 4:8, o:o + w], ALU.mult,
                [XB[k][st] for k in range(4, 8)], [T2b[0], T2b[1]])
            _mm(p, bk.t[:, 0:w], [(ONES, SQ.t[:, k, 0:w]) for k in range(4)], [SQ, CB], [bk], start=True, stop=False)
            _mm(p, bk.t[:, 0:w], [(ONES, SQ2[:, k, 0:w]) for k in range(4)], [T2b[0], T2b[1], CB], [bk], start=False, stop=True)
            _act(p, RSTD.t[:, 0:w], bk.t[:, 0:w], AF.Ln, [bk], [RSTD], bias=DEPS)
            _act(p, RSTD.t[:, 0:w], RSTD.t[:, 0:w], AF.Exp, [RSTD], [RSTD], scale=-0.5)
            BK.put(bk)

        def norm_mod(l, bi, subs, dst, add_on_act=False):
            for (st, o, w) in subs:
                norm_stats(st, o, w)
                if st < 2:
                    def mul_(kc):
                        _stt(p, "dve", T2.t[:, kc % 2, 0:w], XT.t[:, kc, o:o + w], M.t[:, l, (bi + 1) * 8 + kc, 0:1],
                             RSTD.t[:, 0:w], ALU.mult, ALU.mult, [XB[kc][st], MB(l, bi), RSTD], [T2b[kc % 2]])

                    def add_(kc):
                        oap, ob = dst(kc, st, o, w)
                        if add_on_act:
                            _act(p, oap, T2.t[:, kc % 2, 0:w], AF.Identity, [T2b[kc % 2], MB(l, bi)], [ob],
                                 bias=M.t[:, l, bi * 8 + kc, 0:1])
                        else:
                            _ts(p, "dve", oap, T2.t[:, kc % 2, 0:w], M.t[:, l, bi * 8 + kc, 0:1], None, ALU.add, None,
                                [T2b[kc % 2], MB(l, bi)], [ob])

                    mul_(0)
                    for kc in range(1, KC):
                        mul_(kc)
                        add_(kc - 1)
                    add_(KC - 1)
                else:
                    t2 = T2.t[:, 0, 0:KC * NS].rearrange("p (k s) -> p k s", k=KC)
                    rb = RSTD.t[:, 0:NS].unsqueeze(1).broadcast_to([P, KC, NS])
                    xs_b = [XB[k][2] for k in range(KC)]
                    _tt(p, "dve", t2, XT.t[:, :, o:o + NS], rb, ALU.mult, xs_b + [RSTD], [T2b[0]])
                    _tt(p, "dve", t2, t2, M.t[:, l, (bi + 1) * 8:(bi + 2) * 8, 1:17], ALU.mult, [T2b[0], MB(l, bi)], [T2b[0]])
                    oap, obs = dst(None, st, o, w)
                    _tt(p, "dve", oap, t2, M.t[:, l, bi * 8:(bi + 1) * 8, 1:17], ALU.add, [T2b[0], MB(l, bi)], obs)

        def dst_H(kc, st, o, w):
            if kc is None:
                return HT.t[:, :, o:o + w], [HB[k][st] for k in range(KC)]
            return HT.t[:, kc, o:o + w], HB[kc][st]

        def resid_add(l, gi, oc, st, o, w, bk):
            if st < 2:
                _stt(p, "dve", XT.t[:, oc, o:o + w], bk.t[:, 0:w], M.t[:, l, gi * 8 + oc, 0:1], XT.t[:, oc, o:o + w],
                     ALU.mult, ALU.add, [bk, MB(l, gi), XB[oc][st]], [XB[oc][st]])
            else:
                t2 = T2.t[:, 1, 0:NS]
                _tt(p, "dve", t2, bk.t[:, 0:NS], M.t[:, l, gi * 8 + oc, 1:17], ALU.mult, [bk, MB(l, gi)], [T2b[1]])
                _tt(p, "dve", XT.t[:, oc, o:o + NS], XT.t[:, oc, o:o + NS], t2, ALU.add, [T2b[1], XB[oc][st]], [XB[oc][st]])

        def ffn(l, subs, side=None, hoist=None, skip0=False):
            scr.reset(SCR0)
            ACTB = scr.alloc("ACTB", [P, NJ, NCOL], BF16)
            AB = [[Buf("ab%d_%d" % (j, s)) for s in range(3)] for j in range(NJ)]
            SG = scr.alloc("SG", [P, 2, 512], BF16)
            SGb = [Buf("sg0"), Buf("sg1")]
            norm_mod(l, 3, subs[1:] if skip0 else subs, dst_H)
            base, _ = widx[("gu", l)]
            qc = [0]

            def gu(j, r, st, o, w):
                q = qc[0]
                bg = BK.get()
                bu = BK.get()
                hs = [HB[k][st] for k in range(KC)]
                _mm(p, bg.t[:, 0:w], [(blk(r, kc), HT.t[:, kc, o:o + w]) for kc in range(KC)], [r] + hs, [bg])
                _mm(p, bu.t[:, 0:w], [(blk(r, 8 + kc), HT.t[:, kc, o:o + w]) for kc in range(KC)], [r] + hs, [bu])
                _act(p, SG.t[:, q % 2, 0:w], bg.t[:, 0:w], AF.Silu, [bg], [SGb[q % 2]])
                _tt(p, "dve", ACTB.t[:, j, o:o + w], bu.t[:, 0:w], SG.t[:, q % 2, 0:w], ALU.mult,
                    [bu, SGb[q % 2]], [AB[j][st]])
                qc[0] += 1
                BK.put(bg)
                BK.put(bu)

            NB0 = 4
            rs = [ST.next(base + j) for j in range(NB0)]
            for psubs in [subs[0:1], subs[1:]]:
                for j in range(NB0):
                    for (st, o, w) in psubs:
                        gu(j, rs[j], st, o, w)
            for j in range(NB0, NJ):
                if side is not None:
                    next(side, None)
                r = ST.next(base + j)
                for (st, o, w) in subs:
                    gu(j, r, st, o, w)
            base, _ = widx[("dn", l)]
            passes = [subs] if side is not None else [subs[0:1], subs[1:]]
            for pi_, psubs in enumerate(passes):
                cur = None
                curi = -1
                for oc in range(KC):
                    if side is not None:
                        next(side, None)
                    if pi_ == 1 and oc == 2 and hoist is not None:
                        hoist()
                    bks = {st: BK.get() for (st, o, w) in psubs}
                    j = 0
                    while j < NJ:
                        b = oc * NJ + j
                        pi = b // 16
                        if pi != curi:
                            cur = ST.next(base + pi)
                            curi = pi
                        jn = min(NJ, j + (16 - b % 16))
                        for (st, o, w) in psubs:
                            _mm(p, bks[st].t[:, 0:w],
                                [(blk(cur, (oc * NJ + jj) % 16), ACTB.t[:, jj, o:o + w]) for jj in range(j, jn)],
                                [cur] + [AB[jj][st] for jj in range(j, jn)], [bks[st]], start=(j == 0), stop=(jn == NJ))
                        j = jn
                    for (st, o, w) in psubs:
                        resid_add(l, 5, oc, st, o, w, bks[st])
                        BK.put(bks[st])
            if len(passes) == 1 and hoist is not None:
                hoist()
            if side is not None:
                for _ in side:
                    pass
            p.phase_end()

        def pool_mix(l, tile, subs, hoist=None, side=None):
            i = l // 2
            scr.reset(SCR0)
            NZ = 15 + NCOL
            NE = 15 + NTOK
            Z = scr.alloc("Z", [P, KC, NZ], F32)
            ZG = [Buf("zg%d" % g) for g in range(4)]
            A_ = scr.alloc("PA", [P, 2, NE], F32)
            B_ = scr.alloc("PB", [P, 2, NE], F32)
            extra = []
            if tile == 0:
                p.op("dve", lambda e: e.memset(Z.t[:, :, 0:15], 0.0), writes=ZG)
            else:
                _cp(p, "dve", Z.t[:, :, 0:15], HP.t[:, i, :, :], [HPB[i]], ZG)

            def dst_Z(kc, st, o, w):
                if kc is None:
                    return Z.t[:, :, 15 + o:15 + o + w], ZG
                return Z.t[:, kc, 15 + o:15 + o + w], ZG[kc // 2]

            norm_mod(l, 0, subs, dst_Z, add_on_act=True)
            if side is not None:
                for _ in range(12):
                    next(side)
            if tile == 0:
                _cp(p, "dve", HP.t[:, i, :, :], Z.t[:, :, NE - 15:NE], ZG, [HPB[i]])
            else:
                extra.append(p.dma("sp", ppT[i].rearrange("(k p) r -> p k r", p=P), Z.t[:, :, NE - 15:NE],
                                   reads=[ZG[0], ZG[1], ZG[2], ZG[3]], out=True))
            for g in range(4):
                w_ = WIN[g]
                zg = Z.t[:, 2 * g:2 * g + 2, :]
                _tt(p, "dve", A_.t[:, :, 1:NE], zg[:, :, 1:NE], zg[:, :, 0:NE - 1], ALU.add, [ZG[g]], [A_])
                cur = A_
                if w_ >= 4:
                    _tt(p, "dve", B_.t[:, :, 3:NE], A_.t[:, :, 3:NE], A_.t[:, :, 1:NE - 2], ALU.add, [A_], [B_])
                    cur = B_
                if w_ >= 8:
                    _tt(p, "dve", A_.t[:, :, 7:NE], B_.t[:, :, 7:NE], B_.t[:, :, 3:NE - 4], ALU.add, [B_], [A_])
                    cur = A_
                if w_ >= 16:
                    _tt(p, "dve", B_.t[:, :, 15:NE], A_.t[:, :, 15:NE], A_.t[:, :, 7:NE - 8], ALU.add, [A_], [B_])
                    cur = B_
                hw = [HB[2 * g][0], HB[2 * g][1], HB[2 * g + 1][0], HB[2 * g + 1][1]]
                _stt(p, "dve", HT.t[:, 2 * g:2 * g + 2, 0:NTOK], cur.t[:, :, 15:NE], 1.0 / w_, zg[:, :, 15:NE],
                     ALU.mult, ALU.subtract, [cur, ZG[g]], hw)
                if tile == 0:
                    t2 = T2.t[:, 0, 0:32].rearrange("p (k t) -> p k t", k=2)
                    iv = INVC[:, g, :].unsqueeze(1).broadcast_to([P, 2, 16])
                    _tt(p, "dve", t2, cur.t[:, :, 15:31], iv, ALU.mult, [cur, CF], [T2b[0]])
                    _tt(p, "dve", HT.t[:, 2 * g:2 * g + 2, 0:16], t2, zg[:, :, 15:31], ALU.subtract,
                        [T2b[0], ZG[g]], [HB[2 * g][0], HB[2 * g + 1][0]])
            if tile == 1:
                ZH = scr.alloc("ZH", [P, KC, NS, 15], F32)
                ZN = scr.alloc("ZN", [P, KC, NS, 15], F32)
                p.dma("sp", ZH.t.rearrange("p k s r -> p k (s r)"), spT[i].rearrange("(k p) n -> p k n", p=P), writes=[ZH])
                zn = Z.t[:, :, 15 + NTOK:15 + NCOL]
                for g in range(4):
                    w_ = WIN[g]
                    ss = T2.t[:, 1, 0:32].rearrange("p (k s) -> p k s", k=2)
                    p.op("dve", lambda e, ss=ss, g=g, w_=w_: e.tensor_reduce(
                        out=ss, in_=ZH.t[:, 2 * g:2 * g + 2, :, 15 - (w_ - 1):15], axis=AX.X, op=ALU.add),
                        reads=[ZH], writes=[T2b[1]])
                    _tt(p, "dve", ss, ss, zn[:, 2 * g:2 * g + 2, :], ALU.add, [T2b[1], ZG[g]], [T2b[1]])
                    _stt(p, "dve", HT.t[:, 2 * g:2 * g + 2, NTOK:NCOL], ss, 1.0 / w_, zn[:, 2 * g:2 * g + 2, :],
                         ALU.mult, ALU.subtract, [T2b[1], ZG[g]], [HB[2 * g][2], HB[2 * g + 1][2]])
                _cp(p, "dve", ZN.t[:, :, :, 0:14], ZH.t[:, :, :, 1:15], [ZH], [ZN])
                _cp(p, "dve", ZN.t[:, :, :, 14], zn, ZG, [ZN])
                extra.append(p.dma("sp", psT[i].rearrange("(k p) n -> p k n", p=P), ZN.t.rearrange("p k s r -> p k (s r)"),
                                   reads=[ZN], out=True))
            if side is not None:
                for _ in side:
                    pass
            base, _ = widx[("pool", l)]
            r = ST.next(base)
            for si, (st, o, w) in enumerate(subs):
                if si == 1 and hoist is not None:
                    hoist()
                for g in range(4):
                    for oc2 in range(2):
                        oc = 2 * g + oc2
                        bk = BK.get()
                        _mm(p, bk.t[:, 0:w],
                            [(blk(r, g * 4 + oc2 * 2 + kc2), HT.t[:, 2 * g + kc2, o:o + w]) for kc2 in range(2)],
                            [r, HB[2 * g][st], HB[2 * g + 1][st]], [bk])
                        resid_add(l, 2, oc, st, o, w, bk)
                        BK.put(bk)
            p.phase_end(extra)

        def o_post(bo, m, c, og, RA, RAb, SSQ, RST, JNK, tmpb, dst_cols, dst_bufs):
            SSQb, RSTb, JNKb, OGb = tmpb
            for h in range(NH):
                _act(p, JNK.t[0:m, 0:HV], bo[h][1], AF.Square, [bo[h][0]], [JNKb, SSQb], accum=SSQ.t[0:m, h:h + 1])
            yield
            _act(p, RST.t[0:m, :], SSQ.t[0:m, :], AF.Ln, [SSQb], [RSTb], scale=1.0 / HV, bias=float(EPS))
            _act(p, RST.t[0:m, :], RST.t[0:m, :], AF.Exp, [RSTb], [RSTb], scale=-0.5)
            yield
            for h in range(NH):
                _stt(p, "dve", og.t[0:m, h * HV:(h + 1) * HV], bo[h][1], RST.t[0:m, h:h + 1],
                     RA.t[0:m, c, h * HV:(h + 1) * HV], ALU.mult, ALU.mult, [bo[h][0], RSTb, RAb], [OGb])
            yield
            bt = BK.get()
            btb = bt.t[:, :].bitcast(BF16)

            def trs(e):
                ins = None
                for kc in range(KC):
                    ins = e.transpose(btb[:, kc * m:(kc + 1) * m], og.t[0:m, kc * P:(kc + 1) * P], IDB[0:m, 0:m])
                return ins

            p.op("pe", trs, reads=[OGb, CB], writes=[bt])
            yield
            _cp(p, "act", dst_cols, btb[:, 0:KC * m].rearrange("p (k t) -> p k t", k=KC), [bt], dst_bufs)
            BK.put(bt)
            yield

        def run_zip(gens):
            gens = list(gens)
            while gens:
                for g in list(gens):
                    try:
                        next(g)
                    except StopIteration:
                        gens.remove(g)

        def gla_mix(l, tile, subs, hoist=None, skip0=False):
            i = l // 2
            scr.reset(SCR0)
            VA = scr.alloc("VA", [P, 9, 1024], BF16)
            RA = scr.alloc("RA", [P, 9, 1024], BF16)
            VAb = [Buf("va%d" % c) for c in range(9)]
            RAb = [Buf("ra%d" % c) for c in range(9)]
            GLT = scr.alloc("GLT", [P, NCOL], BF16)
            QKS = scr.alloc("QKS", [P, 8, NS], F32)
            OG = scr.alloc("OG", [P, 1024], BF16)
            SBF = scr.alloc("SBF", [P, 2, 1024], BF16)
            SBFb = [Buf("sbf0"), Buf("sbf1")]
            SSQ = scr.alloc("SSQ", [P, 4], F32)
            RST = scr.alloc("RST", [P, 4], F32)
            JNK = scr.alloc("JNK", [P, 256], BF16)
            SGT = scr.alloc("SGT", [P, 2, 512], BF16)
            SGTb = [Buf("sgt0"), Buf("sgt1")]
            tmpb = (SSQ, RST, JNK, OG)
            off_chunk = scr.off
            QK = scr.alloc("QK", [P, 8, NTOK], BF16)
            LAS = scr.alloc("LAS", [P, 512], F32)
            LH = scr.alloc("LH", [P, 512], BF16)
            LL = scr.alloc("LL", [P, 512], BF16)
            EQ = scr.alloc("EQ", [P, 512], F32)
            EK = scr.alloc("EK", [P, 512], F32)
            KS = scr.alloc("KS", [P, 512], BF16)
            KHT = scr.alloc("KHT", [P, 512], BF16)
            QS2 = [scr.alloc("QS%d" % k, [P, 512], BF16) for k in range(2)]
            KH2 = [scr.alloc("KH%d" % k, [P, 512], BF16) for k in range(2)]
            AT2 = [scr.alloc("AT%d" % k, [P, 512], BF16) for k in range(2)]
            EQL2 = [scr.alloc("EQL%d" % k, [P, 4], F32) for k in range(2)]
            extra = []

            norm_mod(l, 0, subs[1:] if skip0 else subs, dst_H)
            _cp(p, "act", SBF.t[:, 0, :], SST.t[:, i, :, :].rearrange("p h v -> p (h v)"), [SSTB[i]], [SBFb[0]])
            p.op("dve", lambda e: e.memset(GLT.t[0:17, :], 1.0), writes=[GLT])

            base, _ = widx[("qk", l)]
            rs = [ST.next(base + pc) for pc in range(N_QK)]
            for psubs in [subs[0:1], subs[1:]]:
                for pc in range(N_QK):
                    r = rs[pc]
                    for o2 in range(2):
                        oc = pc * 2 + o2
                        sc_ = float(HK ** -0.5) if oc < 4 else 1.0
                        for (st, o, w) in psubs:
                            bk = BK.get()
                            _mm(p, bk.t[:, 0:w], [(blk(r, o2 * 8 + kc), HT.t[:, kc, o:o + w]) for kc in range(KC)],
                                [r] + [HB[k][st] for k in range(KC)], [bk])
                            if st < 2:
                                _act(p, QK.t[:, oc, o:o + w], bk.t[:, 0:w], AF.Copy, [bk], [QK], scale=sc_)
                            else:
                                _act(p, QKS.t[:, oc, :], bk.t[:, 0:NS], AF.Copy, [bk], [QKS], scale=sc_)
                            BK.put(bk)
            for (st, o, w) in subs:
                bk = BK.get()
                _mm(p, bk.t[0:GR, 0:w], [(WGL.t[:, i, kc * GR:(kc + 1) * GR], HT.t[:, kc, o:o + w]) for kc in range(KC)],
                    [WGL] + [HB[k][st] for k in range(KC)], [bk])
                _cp(p, "dve", GLT.t[0:GR, o:o + w], bk.t[0:GR, 0:w], [bk], [GLT])
                BK.put(bk)
            base, _ = widx[("vr", l)]
            chunks = list(range(8)) + ([8] if tile == 1 else [])
            q = 0
            for grp in range(4):
                r0 = ST.next(base + 2 * grp)
                r1 = ST.next(base + 2 * grp + 1)
                for c in chunks:
                    if c < 8:
                        m, c0, c1, sti = P, c * P, (c + 1) * P, c // 4
                    else:
                        m, c0, c1, sti = NS, NTOK, NCOL, 2
                    bk = BK.get()
                    _mm(p, bk.t[0:m, :],
                        [(HT.t[:, kc, c0:c1], (r0 if kc < 4 else r1).t[:, (kc % 4) * 512:(kc % 4) * 512 + 512])
                         for kc in range(KC)], [r0, r1] + [HB[k][sti] for k in range(KC)], [bk])
                    if grp < 2:
                        _cp(p, "dve", VA.t[0:m, c, grp * 512:(grp + 1) * 512], bk.t[0:m, :], [bk], [VAb[c]])
                    else:
                        g2 = grp - 2
                        _act(p, SGT.t[0:m, q % 2, :], bk.t[0:m, :], AF.Silu, [bk], [SGTb[q % 2]])
                        gnb = GNR.t[0:m, i * HV:(i + 1) * HV].unsqueeze(1).broadcast_to([m, 2, HV])
                        _tt(p, "dve", RA.t[0:m, c, g2 * 512:(g2 + 1) * 512].rearrange("p (h v) -> p h v", h=2),
                            SGT.t[0:m, q % 2, :].rearrange("p (h v) -> p h v", h=2), gnb, ALU.mult,
                            [SGTb[q % 2], GNR], [RAb[c]])
                        q += 1
                    BK.put(bk)

            Uh = U.unsqueeze(1).broadcast_to([P, NH, P])
            v4 = lambda b: b.t[:, :].rearrange("p (h t) -> p h t", h=NH)

            def prep(c):
                o = c * P
                sl = c % 2
                QS, KH, AT, EQL = QS2[sl], KH2[sl], AT2[sl], EQL2[sl]
                bk1 = BK.get()
                _mm(p, bk1.t[:, :], [(GLT.t[0:17, o:o + P], W2A.t[0:17, i, :])], [GLT, W2A], [bk1])
                yield
                _act(p, LAS.t[:, :], bk1.t[:, :], AF.Exp, [bk1], [LAS], scale=-1.0)
                _act(p, LAS.t[:, :], LAS.t[:, :], AF.Ln, [LAS], [LAS], bias=1.0)
                BK.put(bk1)
                yield
                _cp(p, "dve", LH.t[:, :], LAS.t[:, :], [LAS], [LH])
                _tt(p, "dve", LL.t[:, :], LAS.t[:, :], LH.t[:, :], ALU.subtract, [LAS, LH], [LL])
                yield
                bk2 = BK.get()

                def cums(e, bk2=bk2):
                    ins = None
                    for h in range(NH):
                        e.matmul(bk2.t[:, h * P:(h + 1) * P], LH.t[:, h * P:(h + 1) * P], UB, start=True, stop=False)
                        ins = e.matmul(bk2.t[:, h * P:(h + 1) * P], LL.t[:, h * P:(h + 1) * P], UB, start=False, stop=True)
                    return ins

                p.op("pe", cums, reads=[LH, LL, CB], writes=[bk2])
                yield
                _act(p, EQ.t[:, :], bk2.t[:, :], AF.Exp, [bk2], [EQ], scale=-1.0 / 16)
                _act(p, EK.t[:, :], bk2.t[:, :], AF.Exp, [bk2], [EK], scale=1.0 / 16)
                BK.put(bk2)
                yield
                _cp(p, "dve", EQL.t[:, :], EQ.t[:, P - 1:512:P], [EQ], [EQL])
                _tt(p, "dve", v4(QS), QK.t[:, 0:4, o:o + P], v4(EQ), ALU.mult, [QK, EQ], [QS])
                _tt(p, "dve", v4(KS), QK.t[:, 4:8, o:o + P], v4(EK), ALU.mult, [QK, EK], [KS])
                yield
                for h in range(NH):
                    _stt(p, "dve", KHT.t[:, h * P:(h + 1) * P], QK.t[:, 4 + h, o:o + P], EQL.t[:, h:h + 1],
                         EK.t[:, h * P:(h + 1) * P], ALU.mult, ALU.mult, [QK, EQL, EK], [KHT])
                yield
                bk4 = BK.get()

                def amm(e, bk4=bk4, QS=QS):
                    ins = None
                    for h in range(NH):
                        ins = e.matmul(bk4.t[:, h * P:(h + 1) * P], KS.t[:, h * P:(h + 1) * P], QS.t[:, h * P:(h + 1) * P],
                                       start=True, stop=True)
                    return ins

                p.op("pe", amm, reads=[KS, QS], writes=[bk4])
                yield
                _tt(p, "dve", v4(AT), v4(bk4), Uh, ALU.mult, [bk4, CF], [AT])
                BK.put(bk4)
                yield
                bk3 = BK.get()
                b3 = bk3.t[:, :].bitcast(BF16)

                def trk(e, b3=b3):
                    ins = None
                    for h in range(NH):
                        ins = e.transpose(b3[:, h * P:(h + 1) * P], KHT.t[:, h * P:(h + 1) * P], IDB)
                    return ins

                p.op("pe", trk, reads=[KHT, CB], writes=[bk3])
                yield
                _cp(p, "act", KH.t[:, :], b3[:, 0:512], [bk3], [KH])
                BK.put(bk3)
                yield

            def scan(c):
                o = c * P
                sti = c // 4
                sl = c % 2
                QS, KH, AT, EQL = QS2[sl], KH2[sl], AT2[sl], EQL2[sl]
                bo = [BK.get(), BK.get()]
                cb = c % 2

                def omm(e, bo=bo, c=c, cb=cb, AT=AT, QS=QS):
                    ins = None
                    for h in range(NH):
                        oa = bo[h // 2].t[:, (h % 2) * HV:(h % 2 + 1) * HV]
                        e.matmul(oa, AT.t[:, h * P:(h + 1) * P], VA.t[:, c, h * HV:(h + 1) * HV], start=True, stop=False)
                        ins = e.matmul(oa, QS.t[:, h * P:(h + 1) * P], SBF.t[:, cb, h * HV:(h + 1) * HV], start=False, stop=True)
                    return ins

                p.op("pe", omm, reads=[AT, QS, VAb[c], SBFb[cb]], writes=bo)
                yield
                bs = [BK.get(), BK.get()]

                def smm(e, bs=bs, c=c, KH=KH):
                    ins = None
                    for h in range(NH):
                        sa = bs[h // 2].t[:, (h % 2) * HV:(h % 2 + 1) * HV]
                        ins = e.matmul(sa, KH.t[:, h * P:(h + 1) * P], VA.t[:, c, h * HV:(h + 1) * HV], start=True, stop=True)
                    return ins

                p.op("pe", smm, reads=[KH, VAb[c]], writes=bs)
                yield
                for h in range(NH):
                    _stt(p, "dve", SST.t[:, i, h, :], SST.t[:, i, h, :], EQL.t[:, h:h + 1],
                         bs[h // 2].t[:, (h % 2) * HV:(h % 2 + 1) * HV], ALU.mult, ALU.add,
                         [SSTB[i], EQL, bs[h // 2]], [SSTB[i]])
                BK.put(bs[0])
                BK.put(bs[1])
                yield
                _cp(p, "act", SBF.t[:, 1 - cb, :], SST.t[:, i, :, :].rearrange("p h v -> p (h v)"), [SSTB[i]], [SBFb[1 - cb]])
                yield
                bol = [(bo[h // 2], bo[h // 2].t[:, (h % 2) * HV:(h % 2 + 1) * HV]) for h in range(NH)]
                for _ in o_post(bol, P, c, OG, RA, RAb[c], SSQ, RST, JNK, tmpb, HT.t[:, :, o:o + P], [HB[k][sti] for k in range(KC)]):
                    yield
                BK.put(bo[0])
                BK.put(bo[1])
                yield

            run_zip([prep(0)])
            for c in range(8):
                gp_ = prep(c + 1) if c < 7 else None
                if gp_ is not None:
                    for _ in range(PREP_LEAD):
                        next(gp_, None)
                run_zip(([gp_] if gp_ is not None else []) + [scan(c)])

            if tile == 1:
                p.barrier()
                scr.reset(off_chunk)
                SIN4 = [scr.alloc("SIN%d" % k, [P, NH, HV], F32) for k in range(4)]
                QM = scr.alloc("QM", [P, NH, NS, NS], BF16)
                SINB2 = [scr.alloc("SINB%d" % k, [P, NH, HV], BF16) for k in range(2)]
                AS = scr.alloc("AS", [P, NH * NS], F32)
                bkx = BK.get()

                def xmm(e, bkx=bkx):
                    ins = None
                    for h in range(NH):
                        ins = e.matmul(bkx.t[:, h * NS:(h + 1) * NS], W2A.t[0:17, i, h * P:(h + 1) * P], GLT.t[0:17, NTOK:NCOL],
                                       start=True, stop=True)
                    return ins

                p.op("pe", xmm, reads=[W2A, GLT], writes=[bkx])
                _act(p, AS.t[:, :], bkx.t[:, 0:NH * NS], AF.Exp, [bkx], [AS], scale=-1.0)
                _act(p, AS.t[:, :], AS.t[:, :], AF.Ln, [AS], [AS], bias=1.0)
                _act(p, AS.t[:, :], AS.t[:, :], AF.Exp, [AS], [AS], scale=-1.0 / 16)
                BK.put(bkx)
                _tt(p, "dve", QM.t[:, :, :, :], QKS.t[:, 0:4, :].unsqueeze(2).broadcast_to([P, NH, NS, NS]),
                    I16B.unsqueeze(1).broadcast_to([P, NH, NS, NS]), ALU.mult, [QKS, CF], [QM])
                bos = [BK.get() for _ in range(NH)]
                def load_state(s):
                    SINs = SIN4[s % 4]
                    p.dma("sp", SINs.t[:, :, :], sg[i, s].rearrange("h d v -> d h v"), writes=[SINs])

                for s in range(4):
                    load_state(s)
                for s in range(NS):
                    SIN = SIN4[s % 4]
                    bv = [BK.get(), BK.get()]

                    def vmm(e, bv=bv, s=s):
                        ins = None
                        for hf in range(2):
                            ins = e.matmul(bv[hf].t[:, :], SEL.t[0:NS, s * P:(s + 1) * P], VA.t[0:NS, 8, hf * 512:(hf + 1) * 512],
                                           start=True, stop=True)
                        return ins

                    p.op("pe", vmm, reads=[SEL, VAb[8]], writes=bv)
                    for h in range(NH):
                        _act(p, SIN.t[:, h, :], SIN.t[:, h, :], AF.Copy, [SIN, AS], [SIN], scale=AS.t[:, h * NS + s:h * NS + s + 1])
                    for h in range(NH):
                        _stt(p, "dve", SIN.t[:, h, :], bv[h // 2].t[:, (h % 2) * HV:(h % 2 + 1) * HV],
                             QKS.t[:, 4 + h, s:s + 1], SIN.t[:, h, :], ALU.mult, ALU.add, [bv[h // 2], QKS, SIN], [SIN])
                    BK.put(bv[0])
                    BK.put(bv[1])
                    extra.append(p.dma("pool", gs[i, s].rearrange("h d v -> d h v"), SIN.t[:, :, :], reads=[SIN], out=True))

                    SINB = SINB2[s % 2]
                    _cp(p, "act", SINB.t[:, :, :], SIN.t[:, :, :], [SIN], [SINB])

                    def qmm(e, bos=bos, s=s, SINB=SINB):
                        ins = None
                        for h in range(NH):
                            ins = e.matmul(bos[h].t[0:NS, 0:HV], QM.t[:, h, s, :], SINB.t[:, h, :],
                                           start=(s == 0), stop=(s == NS - 1))
                        return ins

                    p.op("pe", qmm, reads=[QM, SINB], writes=bos)
                    if s + 4 < NS:
                        load_state(s + 4)
                bol = [(bos[h], bos[h].t[0:NS, 0:HV]) for h in range(NH)]
                for _ in o_post(bol, NS, 8, OG, RA, RAb[8], SSQ, RST, JNK, tmpb, HT.t[:, :, NTOK:NCOL], [HB[k][2] for k in range(KC)]):
                    pass
                for h in range(NH):
                    BK.put(bos[h])

            base, _ = widx[("wo", l)]
            for pi_, psubs in enumerate([subs[0:1], subs[1:]]):
                for pc in range(N_WO):
                    if pi_ == 1 and pc == 1 and hoist is not None:
                        hoist()
                    r = ST.next(base + pc)
                    for o2 in range(2):
                        oc = pc * 2 + o2
                        for (st, o, w) in psubs:
                            bk = BK.get()
                            _mm(p, bk.t[:, 0:w], [(blk(r, o2 * 8 + kc), HT.t[:, kc, o:o + w]) for kc in range(KC)],
                                [r] + [HB[k][st] for k in range(KC)], [bk])
                            resid_add(l, 2, oc, st, o, w, bk)
                            BK.put(bk)
            if tile == 1:
                extra.append(p.dma("sp", gp[i].rearrange("h d v -> d h v"), SST.t[:, i, :, :], reads=[SSTB[i]], out=True))
            p.phase_end(extra)

        YT_OFF = SCRW - KC * 512
        fin_extra = []

        YT = Buf("YT", SCRT.t[:, YT_OFF:SCRW].rearrange("p (k t) -> p k t", k=KC))
        YTb = [Buf("yt%d" % k) for k in range(KC)]

        def final_out(tile, subs):
            for (st, o, w) in subs:
                norm_stats(st, o, w)
                if st < 2:
                    for kc in range(KC):
                        y = YT.t[:, kc, 0:w]
                        _stt(p, "dve", y, XT.t[:, kc, o:o + w], PV.t[:, kc, 34:35], RSTD.t[:, 0:w], ALU.mult, ALU.mult,
                             [XB[kc][st], PV, RSTD], [YTb[kc]])
                        fin_extra.append(p.dma("sp", ypT[kc * P:(kc + 1) * P, tile * NTOK + o:tile * NTOK + o + w], y,
                                               reads=[YTb[kc]], out=True))
                else:
                    y = YT.t[:, 0, 0:KC * NS].rearrange("p (k s) -> p k s", k=KC)
                    rb = RSTD.t[:, 0:NS].unsqueeze(1).broadcast_to([P, KC, NS])
                    _tt(p, "dve", y, XT.t[:, :, o:o + NS], rb, ALU.mult, [XB[k][2] for k in range(KC)] + [RSTD], [YTb[0]])
                    _tt(p, "dve", y, y, PV.t[:, :, 34:35].broadcast_to([P, KC, NS]), ALU.mult, [YTb[0], PV], [YTb[0]])
                    fin_extra.append(p.dma("sp", ysT.rearrange("(k p) s -> p k s", p=P), y, reads=[YTb[0]], out=True))

        p.op("dve", lambda e: e.memset(SST.t[:, :, :, :], 0.0), writes=SSTB)
        stage = [0]

        def go():
            stage[0] += 1
            return _DBG_STOP is None or stage[0] <= _DBG_STOP

        for tile in range(2):
            subs = [(0, 0, 512), (1, 512, 512)] + ([(2, NTOK, NS)] if tile == 1 else [])
            if not go():
                break
            for sx in range(2):
                for kc in range(KC):
                    p.dma("sp", XT.t[:, kc, sx * 512:(sx + 1) * 512],
                          xpT[kc * P:(kc + 1) * P, tile * NTOK + sx * 512:tile * NTOK + (sx + 1) * 512], writes=[XB[kc][sx]])
            if tile == 1:
                p.dma("sp", XT.t[:, :, NTOK:NCOL], xsT.rearrange("(k p) s -> p k s", p=P), writes=[XB[k][2] for k in range(KC)])
            gla_hoisted = False
            for l in range(DEPTH):
                def h_ffn(l=l, subs=subs):
                    norm_mod(l, 3, subs[0:1], dst_H)

                if l % 2 == 0:
                    pool_mix(l, tile, subs, hoist=h_ffn, side=(ada_half(0, 1) if (tile == 0 and l == 0) else None))
                else:
                    gla_mix(l, tile, subs, hoist=h_ffn, skip0=gla_hoisted)
                nxt_gla = (l + 1 < DEPTH and (l + 1) % 2 == 1)

                def h_gla(l=l, subs=subs):
                    norm_mod(l + 1, 0, subs[0:1], dst_H)

                def h_fin(tile=tile, subs=subs):
                    final_out(tile, subs[0:1])

                ffn(l, subs, side=(ada_layer(l + 1) if (tile == 0 and l + 1 < DEPTH) else None),
                    hoist=(h_gla if nxt_gla else (h_fin if l == DEPTH - 1 else None)), skip0=True)
                gla_hoisted = nxt_gla
            final_out(tile, subs[1:])
            p.phase_end(fin_extra)
            del fin_extra[:]
        p.finish()
    return nc


_NC_CACHE = {}
_PREP_ONLY = False
_DBG_STOP = None
_DBG_GLA = None


class _Abort(Exception):
    pass


def _gchk(n):
    if _DBG_GLA is not None and n >= _DBG_GLA:
        raise _Abort()


def _consts():
    s = np.arange(P)
    U = (s[:, None] <= s[None, :]).astype(np.float32)
    I16 = np.broadcast_to(np.eye(NS, dtype=np.float32)[None], (P, NS, NS)).reshape(P, NS * NS)
    invc = np.zeros((4, 16), np.float32)
    for g, w in enumerate(WIN):
        invc[g] = 1.0 / np.minimum(np.arange(16) + 1, w)
    INVC = np.broadcast_to(invc.reshape(1, 64), (P, 64))
    cst_f = np.ascontiguousarray(np.concatenate([U, I16, INVC], axis=1), dtype=np.float32)
    cst_b = np.ascontiguousarray(np.concatenate([np.ones((P, P), np.float32), np.eye(P, dtype=np.float32), U], axis=1))
    sel = np.zeros((NS, NS, P), np.float32)
    for i in range(NS):
        sel[i, i, :] = 1.0
    return cst_f, cst_b, sel.reshape(NS, NS * P)


def kernel(x_prompt, x_sample, c_prompt, c_sample, state_pool, state_gla, g_mix, g_ffn,
           w_ada, b_ada, w_pool, s_pool, w_gla_in, w_gla_g2, b_gla_g, g_gla_norm, w_gla_o,
           w_ffn_gate, w_ffn_up, w_ffn_down, g_final):
    f = lambda a: np.asarray(a, dtype=np.float32)
    x_prompt, x_sample, c_prompt, c_sample = f(x_prompt), f(x_sample), f(c_prompt), f(c_sample)
    state_pool, state_gla = f(state_pool), f(state_gla)
    g_mix, g_ffn, w_ada, b_ada, w_pool, s_pool = f(g_mix), f(g_ffn), f(w_ada), f(b_ada), f(w_pool), f(s_pool)
    w_gla_in, w_gla_g2, b_gla_g, g_gla_norm, w_gla_o = f(w_gla_in), f(w_gla_g2), f(b_gla_g), f(g_gla_norm), f(w_gla_o)
    w_ffn_gate, w_ffn_up, w_ffn_down, g_final = f(w_ffn_gate), f(w_ffn_up), f(w_ffn_down), f(g_final)
    NCORES = 8

    wst, _ = build_wstream(w_ada, w_pool, w_gla_in, w_gla_o, w_ffn_gate, w_ffn_up, w_ffn_down)
    pvec = np.zeros((D, NV), np.float32)
    pvec[:, 0:4] = g_mix.T
    pvec[:, 4:8] = g_ffn.T
    for l in range(DEPTH):
        pvec[:, 8 + l * 6:14 + l * 6] = b_ada[l].reshape(6, D).T
    pvec[:, 32:34] = s_pool.T
    pvec[:, 34] = g_final
    w2a = np.concatenate([w_gla_g2, b_gla_g[:, None, :]], axis=1)
    wgl = np.ascontiguousarray(w_gla_in[:, :, 3072:3088].reshape(2, KC, P, GR).transpose(0, 2, 1, 3)).reshape(2, P, KC * GR)
    gnr = np.ascontiguousarray(np.broadcast_to(g_gla_norm.reshape(1, 2 * HV), (P, 2 * HV)))
    cst_f, cst_b, sel = _consts()

    in_maps = []
    for b in range(NCORES):
        ss = slice(b * NS, (b + 1) * NS)
        cT = np.concatenate([c_prompt[b][None, :], c_sample[ss]], axis=0).T
        spT = state_pool[:, ss].transpose(0, 3, 1, 2).reshape(2, D, NS * 15)
        in_maps.append({
            "xpT": np.ascontiguousarray(x_prompt[b].T),
            "xsT": np.ascontiguousarray(x_sample[ss, 0, :].T),
            "cT": np.ascontiguousarray(cT),
            "spT": np.ascontiguousarray(spT),
            "sg": np.ascontiguousarray(state_gla[:, ss]),
            "wst": wst,
            "pvec": pvec,
            "w2a": np.ascontiguousarray(w2a),
            "wgl": wgl,
            "gnr": gnr,
            "cst_f": cst_f,
            "cst_b": cst_b,
            "sel": sel,
        })
    if _PREP_ONLY:
        return in_maps
    if "nc" not in _NC_CACHE:
        _NC_CACHE["nc"] = build_program()
    nc = _NC_CACHE["nc"]
    res = run_bass_kernel_spmd(nc, in_maps, core_ids=list(range(NCORES)))
    R = res.results
    y_prompt = np.stack([R[b]["ypT"].T for b in range(NCORES)], axis=0)
    y_sample = np.concatenate([R[b]["ysT"].T for b in range(NCORES)], axis=0)[:, None, :]
    pool_p = np.stack([R[b]["ppT"].transpose(0, 2, 1) for b in range(NCORES)], axis=1)
    pool_s = np.concatenate([R[b]["psT"].reshape(2, D, NS, 15).transpose(0, 2, 3, 1) for b in range(NCORES)], axis=1)
    gla_p = np.stack([R[b]["gp"] for b in range(NCORES)], axis=1)
    gla_s = np.concatenate([R[b]["gs"] for b in range(NCORES)], axis=1)
    c = lambda a: np.ascontiguousarray(a, dtype=np.float32)
    return (c(y_prompt), c(y_sample), c(pool_p), c(pool_s), c(gla_p), c(gla_s))
```

```python
import numpy as np
import concourse.bass as bass
import concourse.mybir as mybir
from concourse.bass_utils import run_bass_kernel_spmd
from contextlib import ExitStack

F32 = mybir.dt.float32
BF16 = mybir.dt.bfloat16
ALU = mybir.AluOpType
AF = mybir.ActivationFunctionType
AX = mybir.AxisListType
SAME_ENG_GAP = 2


class Buf:
    __slots__ = ("name", "t", "lw", "rd", "sem", "semval", "ssem", "ssemval")

    def __init__(self, name, t=None):
        self.name = name
        self.t = t
        self.lw = None
        self.rd = {}
        self.sem = None
        self.semval = 0
        self.ssem = None
        self.ssemval = 0


class Prog:
    ENG = ("pe", "act", "dve", "pool", "sp")

    def __init__(self, nc, stack):
        self.nc = nc
        self.stack = stack
        self.ops = {e: [] for e in self.ENG}
        self.cnt = {e: 0 for e in self.ENG}
        self.seen = {e: {} for e in self.ENG}
        self.esem = {e: stack.enter_context(nc.semaphore("es_" + e)) for e in self.ENG}
        self.same = {"pe": False, "act": True, "dve": True, "pool": True, "sp": False}
        self.final = []
        self.nsem = 5
        self.sempool = {}

    def sb(self, name, shape, dtype):
        t = self.stack.enter_context(self.nc.sbuf_tensor(name, list(shape), dtype))
        return Buf(name, t)

    def ps(self, name, shape=(128, 512), dtype=F32):
        t = self.stack.enter_context(self.nc.psum_tensor(name, list(shape), dtype))
        return Buf(name, t)

    def newsem(self, name):
        self.nsem += 1
        return self.stack.enter_context(self.nc.semaphore(name))

    def _waits(self, eng, reads, writes, skip_dma_waw=None):
        own = self.esem[eng]
        deps = []
        for b in reads:
            if b.lw is not None:
                if b.lw[0] is own and (not self.same[eng] or (SAME_ENG_GAP is not None and self.cnt[eng] - b.lw[1] >= SAME_ENG_GAP)):
                    continue
                deps.append(b.lw)
        for b in writes:
            if b.lw is not None and not (skip_dma_waw is not None and b.lw[0] is skip_dma_waw):
                if b.lw[0] is not own or (SAME_ENG_GAP is None and self.same[eng]):
                    deps.append(b.lw)
            for d in b.rd.values():
                if d[0] is not own or (SAME_ENG_GAP is None and self.same[eng]):
                    deps.append(d)
        seen = self.seen[eng]
        best = {}
        for sem, val in deps:
            k = id(sem)
            if seen.get(k, 0) >= val:
                continue
            if k not in best or best[k][1] < val:
                best[k] = (sem, val)
        for k, (sem, val) in best.items():
            seen[k] = val
        return list(best.values())

    def op(self, eng, fn, reads=(), writes=()):
        waits = self._waits(eng, reads, writes)
        self.cnt[eng] += 1
        tick = (self.esem[eng], self.cnt[eng])
        self.ops[eng].append((waits, fn, (self.esem[eng], 1)))
        for b in reads:
            b.rd[id(tick[0])] = tick
        for b in writes:
            b.lw = tick
            b.rd = {}
        return tick

    def dma(self, q, out_ap, in_ap, reads=(), writes=(), out=False, **kw):
        if writes:
            key = "ld_" + writes[0].name
        elif reads:
            key = "st_" + reads[0].name
        else:
            key = "outsem"
        if key not in self.sempool:
            self.sempool[key] = [self.newsem(key), 0]
        ent = self.sempool[key]
        sem = ent[0]
        ent[1] += 16
        val = ent[1]
        waits = self._waits(q, reads, writes, skip_dma_waw=sem)
        tok = (sem, val)

        def fn(e, out_ap=out_ap, in_ap=in_ap, kw=kw):
            return e.dma_start(out=out_ap, in_=in_ap, **kw)

        self.ops[q].append((waits, fn, (sem, 16)))
        for b in reads:
            b.rd[id(sem)] = tok
        for b in writes:
            b.lw = tok
            b.rd = {}
        if out:
            for i, (s, v) in enumerate(self.final):
                if s is sem:
                    self.final[i] = tok
                    break
            else:
                self.final.append(tok)
        return tok

    def barrier(self, extra=()):
        toks = [(self.esem[e], self.cnt[e]) for e in self.ENG if self.cnt[e] > 0] + list(extra)
        for e in self.ENG:
            waits = []
            for sem, val in toks:
                if sem is self.esem[e]:
                    continue
                if self.seen[e].get(id(sem), 0) >= val:
                    continue
                self.seen[e][id(sem)] = val
                waits.append((sem, val))
            if waits:
                self.ops[e].append((waits, None, None))

    def phase_end(self, extra=()):
        toks = [(self.esem[e], self.cnt[e]) for e in self.ENG if self.cnt[e] > 0]
        for e in self.ENG:
            waits = []
            for sem, val in (toks if e == "sp" else []) + list(extra):
                if sem is self.esem[e]:
                    continue
                if self.seen[e].get(id(sem), 0) >= val:
                    continue
                self.seen[e][id(sem)] = val
                waits.append((sem, val))
            if waits:
                self.ops[e].append((waits, None, None))

    def finish(self):
        waits = list(self.final)
        for e in self.ENG:
            if e != "sp" and self.cnt[e] > 0:
                waits.append((self.esem[e], self.cnt[e]))
        self.ops["sp"].append((waits, None, None))
        nc = self.nc
        ops = self.ops

        def run(name, e):
            for waits, fn, inc in ops[name]:
                for sem, val in waits:
                    e.wait_ge(sem, val)
                if fn is not None:
                    ins = fn(e)
                    ins.then_inc(inc[0], inc[1])

        with nc.Block() as block:
            @block.tensor
            def _(e):
                run("pe", e)

            @block.scalar
            def _(e):
                run("act", e)

            @block.vector
            def _(e):
                run("dve", e)

            @block.gpsimd
            def _(e):
                run("pool", e)

            @block.sync
            def _(e):
                run("sp", e)


P = 128
D = 1024
KC = 8
SEQ = 2048
NTOK = 1024
NS = 16
NCOL = NTOK + NS
DEPTH = 4
FF = 2816
NJ = FF // P
HK = 128
HV = 256
NH = 4
GR = 16
EPS = 1e-6
NV = 35
RING = 6
PREP_LEAD = 1
SCRW = 25500
WIN = (2, 4, 8, 16)

N_ADA = 24
N_POOL = 1
N_QK, N_VR, N_WO = 4, 8, 4
N_GU, N_DN = 22, 11


def _pieces_from_blocks(blk):
    nb = blk.shape[0]
    assert nb % 16 == 0
    return np.ascontiguousarray(blk.reshape(nb // 16, 16, P, P).transpose(0, 2, 1, 3)).reshape(nb // 16, P, 16 * P)


def _oc_major(W):
    K, N = W.shape
    return W.reshape(K // P, P, N // P, P).transpose(2, 0, 1, 3).reshape(-1, P, P)


def build_wstream(w_ada, w_pool, w_gla_in, w_gla_o, w_ffn_gate, w_ffn_up, w_ffn_down):
    pcs = []
    index = {}
    pos = 0

    def add(key, arr):
        nonlocal pos
        index[key] = (pos, arr.shape[0])
        pcs.append(arr)
        pos += arr.shape[0]

    for l in range(DEPTH):
        add(("ada", l), _pieces_from_blocks(_oc_major(w_ada[l])))
    for l in range(DEPTH):
        i = l // 2
        if l % 2 == 0:
            wp = w_pool[i]
            blk = wp.reshape(4, 2, P, 2, P).transpose(0, 3, 1, 2, 4).reshape(16, P, P)
            add(("pool", l), _pieces_from_blocks(blk))
        else:
            W = w_gla_in[i]
            add(("qk", l), _pieces_from_blocks(_oc_major(W[:, 0:1024])))
            Wvr = W[:, 1024:3072]
            blk = Wvr.reshape(KC, P, 4, 4, P).transpose(2, 0, 3, 1, 4).reshape(128, P, P)
            add(("vr", l), _pieces_from_blocks(blk))
            add(("wo", l), _pieces_from_blocks(_oc_major(w_gla_o[i])))
        g = w_ffn_gate[l].reshape(KC, P, NJ, P).transpose(2, 0, 1, 3)
        u = w_ffn_up[l].reshape(KC, P, NJ, P).transpose(2, 0, 1, 3)
        blk = np.concatenate([g, u], axis=1).reshape(NJ * 16, P, P)
        add(("gu", l), _pieces_from_blocks(blk))
        dn = w_ffn_down[l].reshape(NJ, P, KC, P).transpose(2, 0, 1, 3).reshape(KC * NJ, P, P)
        add(("dn", l), _pieces_from_blocks(dn))
    return np.concatenate(pcs, axis=0), index


def wstream_index():
    index = {}
    pos = 0
    for l in range(DEPTH):
        index[("ada", l)] = (pos, N_ADA)
        pos += N_ADA
    for l in range(DEPTH):
        if l % 2 == 0:
            index[("pool", l)] = (pos, N_POOL); pos += N_POOL
        else:
            index[("qk", l)] = (pos, N_QK); pos += N_QK
            index[("vr", l)] = (pos, N_VR); pos += N_VR
            index[("wo", l)] = (pos, N_WO); pos += N_WO
        index[("gu", l)] = (pos, N_GU); pos += N_GU
        index[("dn", l)] = (pos, N_DN); pos += N_DN
    return index, pos


class Scratch:
    def __init__(self, t, nwords):
        self.t = t
        self.n = nwords
        self.off = 0

    def reset(self, off=0):
        self.off = off

    def alloc(self, name, shape, dtype):
        nel = 1
        for s in shape[1:]:
            nel *= s
        words = nel if dtype == F32 else (nel + 1) // 2
        ap = self.t[:, self.off:self.off + words]
        self.off += words
        assert self.off <= self.n, (name, self.off, self.n)
        if dtype == BF16:
            ap = ap.bitcast(BF16)
            if nel % 2:
                ap = ap[:, 0:nel]
        if len(shape) == 3:
            ap = ap.rearrange("p (a b) -> p a b", a=shape[1])
        elif len(shape) == 4:
            ap = ap.rearrange("p (a b c) -> p a b c", a=shape[1], b=shape[2])
        if shape[0] < P:
            ap = ap[0:shape[0]]
        return Buf(name, ap)


def _tt(p, eng, out, in0, in1, op, R, W):
    p.op(eng, lambda e: e.tensor_tensor(out=out, in0=in0, in1=in1, op=op), reads=R, writes=W)


def _stt(p, eng, out, in0, scalar, in1, op0, op1, R, W):
    p.op(eng, lambda e: e.scalar_tensor_tensor(out=out, in0=in0, scalar=scalar, in1=in1, op0=op0, op1=op1),
         reads=R, writes=W)


def _ts(p, eng, out, in0, s1, s2, op0, op1, R, W):
    if s2 is None:
        p.op(eng, lambda e: e.tensor_scalar(out=out, in0=in0, scalar1=s1, scalar2=None, op0=op0), reads=R, writes=W)
    else:
        p.op(eng, lambda e: e.tensor_scalar(out=out, in0=in0, scalar1=s1, scalar2=s2, op0=op0, op1=op1),
             reads=R, writes=W)


def _act(p, out, in_, func, R, W, bias=None, scale=None, accum=None):
    kw = {}
    if bias is not None:
        kw["bias"] = bias
    if scale is not None:
        kw["scale"] = scale
    if accum is not None:
        kw["accum_out"] = accum
    p.op("act", lambda e: e.activation(out=out, in_=in_, func=func, **kw), reads=R, writes=W)


def _cp(p, eng, out, in_, R, W):
    if eng == "act":
        p.op("act", lambda e: e.activation(out=out, in_=in_, func=AF.Copy), reads=R, writes=W)
    else:
        p.op(eng, lambda e: e.tensor_copy(out=out, in_=in_), reads=R, writes=W)


def _mm(p, out, pairs, R, W, start=True, stop=True):
    pairs = list(pairs)

    def fn(e):
        ins = None
        n = len(pairs)
        for i, (l, r) in enumerate(pairs):
            ins = e.matmul(out, l, r, start=(start and i == 0), stop=(stop and i == n - 1))
        return ins

    p.op("pe", fn, reads=R, writes=W)


def _tr(p, out, in_, ident, R, W):
    p.op("pe", lambda e: e.transpose(out, in_, ident), reads=R, writes=W)


class Banks:
    def __init__(self, p, n=8):
        self.free = [p.ps("pb%d" % i) for i in range(n)]

    def get(self):
        return self.free.pop(0)

    def put(self, b):
        self.free.append(b)


class Stream:
    def __init__(self, p, wst, ring):
        self.p = p
        self.wst = wst
        self.ring = ring
        self.seq = 0

    def next(self, idx):
        r = self.ring[self.seq % len(self.ring)]
        self.seq += 1
        self.p.dma("pool", r.t[:, :], self.wst[idx], writes=[r])
        return r


def blk(r, b, n=1):
    return r.t[:, b * P:(b + n) * P]


def build_program():
    nc = bass.Bass("TRN2", target_bir_lowering=False)
    widx, NPIECE = wstream_index()

    def din(name, shape):
        return nc.dram_tensor(name, list(shape), F32, kind="ExternalInput").ap()

    def dout(name, shape):
        return nc.dram_tensor(name, list(shape), F32, kind="ExternalOutput").ap()

    xpT = din("xpT", [D, SEQ])
    xsT = din("xsT", [D, NS])
    cT = din("cT", [D, 17])
    spT = din("spT", [2, D, NS * 15])
    sg = din("sg", [2, NS, NH, HK, HV])
    wst = din("wst", [NPIECE, P, 2048])
    pvec = din("pvec", [D, NV])
    w2a = din("w2a", [2, 17, 512])
    wgl = din("wgl", [2, P, KC * GR])
    gnr = din("gnr", [P, 2 * HV])
    cst_f = din("cst_f", [P, 128 + 256 + 64])
    cst_b = din("cst_b", [P, 384])
    sel = din("sel", [NS, NS * P])
    ypT = dout("ypT", [D, SEQ])
    ysT = dout("ysT", [D, NS])
    ppT = dout("ppT", [2, D, 15])
    psT = dout("psT", [2, D, NS * 15])
    gp = dout("gp", [2, NH, HK, HV])
    gs = dout("gs", [2, NS, NH, HK, HV])

    with ExitStack() as stack:
        p = Prog(nc, stack)
        BK = Banks(p)
        XT = p.sb("XT", [P, KC, NCOL], F32)
        HT = p.sb("HT", [P, KC, NCOL], BF16)
        XB = [[Buf("x%d_%d" % (k, s)) for s in range(3)] for k in range(KC)]
        HB = [[Buf("h%d_%d" % (k, s)) for s in range(3)] for k in range(KC)]
        ring = [p.sb("ring%d" % i, [P, 2048], BF16) for i in range(RING)]
        ST = Stream(p, wst, ring)
        SST = p.sb("SST", [P, 2, NH, HV], F32)
        SSTB = [Buf("sst0"), Buf("sst1")]
        M = p.sb("M", [P, DEPTH, 48, 17], F32)
        Ml = [Buf("M%d" % l) for l in range(DEPTH)]
        Mf = [Buf("Mf%d" % l) for l in range(DEPTH)]

        def MB(l, idx):
            return Ml[l] if idx < 3 else Mf[l]

        HP = p.sb("HP", [P, 2, KC, 15], F32)
        HPB = [Buf("hp0"), Buf("hp1")]
        PV = p.sb("PV", [P, KC, NV], F32)
        CF = p.sb("CF", [P, 448], F32)
        CB = p.sb("CB", [P, 384], BF16)
        SEL = p.sb("SEL", [NS, NS * P], BF16)
        W2A = p.sb("W2A", [17, 2, 512], BF16)
        WGL = p.sb("WGL", [P, 2, KC * GR], BF16)
        GNR = p.sb("GNR", [P, 2 * HV], F32)
        SCB = p.sb("SCB", [P, KC, 17], BF16)
        SCRT = p.sb("SCR", [P, SCRW], F32)
        scr = Scratch(SCRT.t, SCRW)
        U = CF.t[:, 0:128]
        I16B = CF.t[:, 128:384].rearrange("p (a b) -> p a b", a=16)
        INVC = CF.t[:, 384:448].rearrange("p (g t) -> p g t", g=4)
        ONES = CB.t[:, 0:128]
        IDB = CB.t[:, 128:256]
        UB = CB.t[:, 256:384]
        DEPS = float(D * EPS)

        SQ = scr.alloc("SQ", [P, 4, 512], BF16)
        T2 = scr.alloc("T2", [P, 2, 512], F32)
        T2b = [Buf("t2a"), Buf("t2b")]
        RSTD = scr.alloc("RSTD", [P, 512], F32)
        SCR0 = scr.off

        p.dma("sp", PV.t[:, :, :], pvec.rearrange("(k p) n -> p k n", p=P), writes=[PV])
        p.dma("sp", CF.t[:, :], cst_f, writes=[CF])
        p.dma("pool", CB.t[:, :], cst_b, writes=[CB])
        p.dma("pool", SEL.t[:, :], sel, writes=[SEL])
        p.dma("pool", W2A.t[:, :, :], w2a.rearrange("i r c -> r i c"), writes=[W2A])
        p.dma("pool", WGL.t[:, :, :], wgl.rearrange("i p n -> p i n"), writes=[WGL])
        p.dma("sp", GNR.t[:, :], gnr, writes=[GNR])
        _ts(p, "dve", PV.t[:, :, 0:8], PV.t[:, :, 0:8], 32.0, None, ALU.mult, None, [PV], [PV])
        _ts(p, "dve", PV.t[:, :, 34:35], PV.t[:, :, 34:35], 32.0, None, ALU.mult, None, [PV], [PV])

        scr.reset(SCR0)
        SCF = scr.alloc("SCF", [P, KC, 17], F32)
        p.dma("sp", SCF.t[:, :, :], cT.rearrange("(k p) r -> p k r", p=P), writes=[SCF])
        _act(p, SCB.t[:, :, :], SCF.t[:, :, :], AF.Silu, [SCF], [SCB])

        def ada_half(l, hf):
            base, n = widx[("ada", l)]
            bk = BK.get()
            mb = MB(l, 3 * hf)
            for pc in range(hf * 12, hf * 12 + 12):
                r = ST.next(base + pc)
                for o2 in range(2):
                    oc = pc * 2 + o2
                    c0 = (oc - hf * 24) * 17
                    _mm(p, bk.t[:, c0:c0 + 17], [(blk(r, o2 * 8 + kc), SCB.t[:, kc, :]) for kc in range(KC)],
                        [r, SCB], [bk])
                yield
            _cp(p, "dve", M.t[:, l, hf * 24:hf * 24 + 24, :], bk.t[:, 0:408].rearrange("p (a b) -> p a b", b=17), [bk], [mb])
            BK.put(bk)
            Mv = M.t[:, l, hf * 24:hf * 24 + 24, :].rearrange("p (i k) r -> p i k r", i=3)
            c0 = 8 + l * 6 + hf * 3
            bv = PV.t[:, :, c0:c0 + 3].rearrange("p k i -> p i k").unsqueeze(3).broadcast_to([P, 3, KC, 17])
            _tt(p, "dve", Mv, Mv, bv, ALU.add, [mb, PV], [mb])
            idx, col = (1, l) if hf == 0 else (4, 4 + l)
            mv = M.t[:, l, idx * 8:(idx + 1) * 8, :]
            gv = PV.t[:, :, col:col + 1].broadcast_to([P, KC, 17])
            _stt(p, "dve", mv, mv, 1.0, gv, ALU.add, ALU.mult, [mb, PV], [mb])
            if hf == 0 and l % 2 == 0:
                mv = M.t[:, l, 16:24, :]
                sv = PV.t[:, :, 32 + l // 2:33 + l // 2].broadcast_to([P, KC, 17])
                _tt(p, "dve", mv, mv, sv, ALU.mult, [mb, PV], [mb])
            yield

        def ada_layer(l):
            for hf in range(2):
                for _ in ada_half(l, hf):
                    yield

        for _ in ada_half(0, 0):
            pass
        p.barrier()

        SQ2 = T2.t[:, :, :].rearrange("p a b -> p (a b)").bitcast(BF16).rearrange("p (k t) -> p k t", k=4)

        def norm_stats(st, o, w):
            bk = BK.get()
            _act(p, SQ.t[:, :, 0:w], XT.t[:, 0:4, o:o + w], AF.Square, [XB[k][st] for k in range(4)], [SQ])
            _tt(p, "dve", SQ2[:, :, 0:w], XT.t[:, 4:8, o:o + w], XT.t[:, 4:8, o:o + w], ALU.mult,
                [XB[k][st] for k in range(4, 8)], [T2b[0], T2b[1]])
            _mm(p, bk.t[:, 0:w], [(ONES, SQ.t[:, k, 0:w]) for k in range(4)], [SQ, CB], [bk], start=True, stop=False)
            _mm(p, bk.t[:, 0:w], [(ONES, SQ2[:, k, 0:w]) for k in range(4)], [T2b[0], T2b[1], CB], [bk], start=False, stop=True)
            _act(p, RSTD.t[:, 0:w], bk.t[:, 0:w], AF.Ln, [bk], [RSTD], bias=DEPS)
            _act(p, RSTD.t[:, 0:w], RSTD.t[:, 0:w], AF.Exp, [RSTD], [RSTD], scale=-0.5)
            BK.put(bk)

        def norm_mod(l, bi, subs, dst, add_on_act=False):
            for (st, o, w) in subs:
                norm_stats(st, o, w)
                if st < 2:
                    def mul_(kc):
                        _stt(p, "dve", T2.t[:, kc % 2, 0:w], XT.t[:, kc, o:o + w], M.t[:, l, (bi + 1) * 8 + kc, 0:1],
                             RSTD.t[:, 0:w], ALU.mult, ALU.mult, [XB[kc][st], MB(l, bi), RSTD], [T2b[kc % 2]])

                    def add_(kc):
                        oap, ob = dst(kc, st, o, w)
                        if add_on_act:
                            _act(p, oap, T2.t[:, kc % 2, 0:w], AF.Identity, [T2b[kc % 2], MB(l, bi)], [ob],
                                 bias=M.t[:, l, bi * 8 + kc, 0:1])
                        else:
                            _ts(p, "dve", oap, T2.t[:, kc % 2, 0:w], M.t[:, l, bi * 8 + kc, 0:1], None, ALU.add, None,
                                [T2b[kc % 2], MB(l, bi)], [ob])

                    mul_(0)
                    for kc in range(1, KC):
                        mul_(kc)
                        add_(kc - 1)
                    add_(KC - 1)
                else:
                    t2 = T2.t[:, 0, 0:KC * NS].rearrange("p (k s) -> p k s", k=KC)
                    rb = RSTD.t[:, 0:NS].unsqueeze(1).broadcast_to([P, KC, NS])
                    xs_b = [XB[k][2] for k in range(KC)]
                    _tt(p, "dve", t2, XT.t[:, :, o:o + NS], rb, ALU.mult, xs_b + [RSTD], [T2b[0]])
                    _tt(p, "dve", t2, t2, M.t[:, l, (bi + 1) * 8:(bi + 2) * 8, 1:17], ALU.mult, [T2b[0], MB(l, bi)], [T2b[0]])
                    oap, obs = dst(None, st, o, w)
                    _tt(p, "dve", oap, t2, M.t[:, l, bi * 8:(bi + 1) * 8, 1:17], ALU.add, [T2b[0], MB(l, bi)], obs)

        def dst_H(kc, st, o, w):
            if kc is None:
                return HT.t[:, :, o:o + w], [HB[k][st] for k in range(KC)]
            return HT.t[:, kc, o:o + w], HB[kc][st]

        def resid_add(l, gi, oc, st, o, w, bk):
            if st < 2:
                _stt(p, "dve", XT.t[:, oc, o:o + w], bk.t[:, 0:w], M.t[:, l, gi * 8 + oc, 0:1], XT.t[:, oc, o:o + w],
                     ALU.mult, ALU.add, [bk, MB(l, gi), XB[oc][st]], [XB[oc][st]])
            else:
                t2 = T2.t[:, 1, 0:NS]
                _tt(p, "dve", t2, bk.t[:, 0:NS], M.t[:, l, gi * 8 + oc, 1:17], ALU.mult, [bk, MB(l, gi)], [T2b[1]])
                _tt(p, "dve", XT.t[:, oc, o:o + NS], XT.t[:, oc, o:o + NS], t2, ALU.add, [T2b[1], XB[oc][st]], [XB[oc][st]])

        def ffn(l, subs, side=None, hoist=None, skip0=False):
            scr.reset(SCR0)
            ACTB = scr.alloc("ACTB", [P, NJ, NCOL], BF16)
            AB = [[Buf("ab%d_%d" % (j, s)) for s in range(3)] for j in range(NJ)]
            SG = scr.alloc("SG", [P, 2, 512], BF16)
            SGb = [Buf("sg0"), Buf("sg1")]
            norm_mod(l, 3, subs[1:] if skip0 else subs, dst_H)
            base, _ = widx[("gu", l)]
            qc = [0]

            def gu(j, r, st, o, w):
                q = qc[0]
                bg = BK.get()
                bu = BK.get()
                hs = [HB[k][st] for k in range(KC)]
                _mm(p, bg.t[:, 0:w], [(blk(r, kc), HT.t[:, kc, o:o + w]) for kc in range(KC)], [r] + hs, [bg])
                _mm(p, bu.t[:, 0:w], [(blk(r, 8 + kc), HT.t[:, kc, o:o + w]) for kc in range(KC)], [r] + hs, [bu])
                _act(p, SG.t[:, q % 2, 0:w], bg.t[:, 0:w], AF.Silu, [bg], [SGb[q % 2]])
                _tt(p, "dve", ACTB.t[:, j, o:o + w], bu.t[:, 0:w], SG.t[:, q % 2, 0:w], ALU.mult,
                    [bu, SGb[q % 2]], [AB[j][st]])
                qc[0] += 1
                BK.put(bg)
                BK.put(bu)

            NB0 = 4
            rs = [ST.next(base + j) for j in range(NB0)]
            for psubs in [subs[0:1], subs[1:]]:
                for j in range(NB0):
                    for (st, o, w) in psubs:
                        gu(j, rs[j], st, o, w)
            for j in range(NB0, NJ):
                if side is not None:
                    next(side, None)
                r = ST.next(base + j)
                for (st, o, w) in subs:
                    gu(j, r, st, o, w)
            base, _ = widx[("dn", l)]
            passes = [subs] if side is not None else [subs[0:1], subs[1:]]
            for pi_, psubs in enumerate(passes):
                cur = None
                curi = -1
                for oc in range(KC):
                    if side is not None:
                        next(side, None)
                    if pi_ == 1 and oc == 2 and hoist is not None:
                        hoist()
                    bks = {st: BK.get() for (st, o, w) in psubs}
                    j = 0
                    while j < NJ:
                        b = oc * NJ + j
                        pi = b // 16
                        if pi != curi:
                            cur = ST.next(base + pi)
                            curi = pi
                        jn = min(NJ, j + (16 - b % 16))
                        for (st, o, w) in psubs:
                            _mm(p, bks[st].t[:, 0:w],
                                [(blk(cur, (oc * NJ + jj) % 16), ACTB.t[:, jj, o:o + w]) for jj in range(j, jn)],
                                [cur] + [AB[jj][st] for jj in range(j, jn)], [bks[st]], start=(j == 0), stop=(jn == NJ))
                        j = jn
                    for (st, o, w) in psubs:
                        resid_add(l, 5, oc, st, o, w, bks[st])
                        BK.put(bks[st])
            if len(passes) == 1 and hoist is not None:
                hoist()
            if side is not None:
                for _ in side:
                    pass
            p.phase_end()

        def pool_mix(l, tile, subs, hoist=None, side=None):
            i = l // 2
            scr.reset(SCR0)
            NZ = 15 + NCOL
            NE = 15 + NTOK
            Z = scr.alloc("Z", [P, KC, NZ], F32)
            ZG = [Buf("zg%d" % g) for g in range(4)]
            A_ = scr.alloc("PA", [P, 2, NE], F32)
            B_ = scr.alloc("PB", [P, 2, NE], F32)
            extra = []
            if tile == 0:
                p.op("dve", lambda e: e.memset(Z.t[:, :, 0:15], 0.0), writes=ZG)
            else:
                _cp(p, "dve", Z.t[:, :, 0:15], HP.t[:, i, :, :], [HPB[i]], ZG)

            def dst_Z(kc, st, o, w):
                if kc is None:
                    return Z.t[:, :, 15 + o:15 + o + w], ZG
                return Z.t[:, kc, 15 + o:15 + o + w], ZG[kc // 2]

            norm_mod(l, 0, subs, dst_Z, add_on_act=True)
            if side is not None:
                for _ in range(12):
                    next(side)
            if tile == 0:
                _cp(p, "dve", HP.t[:, i, :, :], Z.t[:, :, NE - 15:NE], ZG, [HPB[i]])
            else:
                extra.append(p.dma("sp", ppT[i].rearrange("(k p) r -> p k r", p=P), Z.t[:, :, NE - 15:NE],
                                   reads=[ZG[0], ZG[1], ZG[2], ZG[3]], out=True))
            for g in range(4):
                w_ = WIN[g]
                zg = Z.t[:, 2 * g:2 * g + 2, :]
                _tt(p, "dve", A_.t[:, :, 1:NE], zg[:, :, 1:NE], zg[:, :, 0:NE - 1], ALU.add, [ZG[g]], [A_])
                cur = A_
                if w_ >= 4:
                    _tt(p, "dve", B_.t[:, :, 3:NE], A_.t[:, :, 3:NE], A_.t[:, :, 1:NE - 2], ALU.add, [A_], [B_])
                    cur = B_
                if w_ >= 8:
                    _tt(p, "dve", A_.t[:, :, 7:NE], B_.t[:, :, 7:NE], B_.t[:, :, 3:NE - 4], ALU.add, [B_], [A_])
                    cur = A_
                if w_ >= 16:
                    _tt(p, "dve", B_.t[:, :, 15:NE], A_.t[:, :, 15:NE], A_.t[:, :, 7:NE - 8], ALU.add, [A_], [B_])
                    cur = B_
                hw = [HB[2 * g][0], HB[2 * g][1], HB[2 * g + 1][0], HB[2 * g + 1][1]]
                _stt(p, "dve", HT.t[:, 2 * g:2 * g + 2, 0:NTOK], cur.t[:, :, 15:NE], 1.0 / w_, zg[:, :, 15:NE],
                     ALU.mult, ALU.subtract, [cur, ZG[g]], hw)
                if tile == 0:
                    t2 = T2.t[:, 0, 0:32].rearrange("p (k t) -> p k t", k=2)
                    iv = INVC[:, g, :].unsqueeze(1).broadcast_to([P, 2, 16])
                    _tt(p, "dve", t2, cur.t[:, :, 15:31], iv, ALU.mult, [cur, CF], [T2b[0]])
                    _tt(p, "dve", HT.t[:, 2 * g:2 * g + 2, 0:16], t2, zg[:, :, 15:31], ALU.subtract,
                        [T2b[0], ZG[g]], [HB[2 * g][0], HB[2 * g + 1][0]])
            if tile == 1:
                ZH = scr.alloc("ZH", [P, KC, NS, 15], F32)
                ZN = scr.alloc("ZN", [P, KC, NS, 15], F32)
                p.dma("sp", ZH.t.rearrange("p k s r -> p k (s r)"), spT[i].rearrange("(k p) n -> p k n", p=P), writes=[ZH])
                zn = Z.t[:, :, 15 + NTOK:15 + NCOL]
                for g in range(4):
                    w_ = WIN[g]
                    ss = T2.t[:, 1, 0:32].rearrange("p (k s) -> p k s", k=2)
                    p.op("dve", lambda e, ss=ss, g=g, w_=w_: e.tensor_reduce(
                        out=ss, in_=ZH.t[:, 2 * g:2 * g + 2, :, 15 - (w_ - 1):15], axis=AX.X, op=ALU.add),
                        reads=[ZH], writes=[T2b[1]])
                    _tt(p, "dve", ss, ss, zn[:, 2 * g:2 * g + 2, :], ALU.add, [T2b[1], ZG[g]], [T2b[1]])
                    _stt(p, "dve", HT.t[:, 2 * g:2 * g + 2, NTOK:NCOL], ss, 1.0 / w_, zn[:, 2 * g:2 * g + 2, :],
                         ALU.mult, ALU.subtract, [T2b[1], ZG[g]], [HB[2 * g][2], HB[2 * g + 1][2]])
                _cp(p, "dve", ZN.t[:, :, :, 0:14], ZH.t[:, :, :, 1:15], [ZH], [ZN])
                _cp(p, "dve", ZN.t[:, :, :, 14], zn, ZG, [ZN])
                extra.append(p.dma("sp", psT[i].rearrange("(k p) n -> p k n", p=P), ZN.t.rearrange("p k s r -> p k (s r)"),
                                   reads=[ZN], out=True))
            if side is not None:
                for _ in side:
                    pass
            base, _ = widx[("pool", l)]
            r = ST.next(base)
            for si, (st, o, w) in enumerate(subs):
                if si == 1 and hoist is not None:
                    hoist()
                for g in range(4):
                    for oc2 in range(2):
                        oc = 2 * g + oc2
                        bk = BK.get()
                        _mm(p, bk.t[:, 0:w],
                            [(blk(r, g * 4 + oc2 * 2 + kc2), HT.t[:, 2 * g + kc2, o:o + w]) for kc2 in range(2)],
                            [r, HB[2 * g][st], HB[2 * g + 1][st]], [bk])
                        resid_add(l, 2, oc, st, o, w, bk)
                        BK.put(bk)
            p.phase_end(extra)

        def o_post(bo, m, c, og, RA, RAb, SSQ, RST, JNK, tmpb, dst_cols, dst_bufs):
            SSQb, RSTb, JNKb, OGb = tmpb
            for h in range(NH):
                _act(p, JNK.t[0:m, 0:HV], bo[h][1], AF.Square, [bo[h][0]], [JNKb, SSQb], accum=SSQ.t[0:m, h:h + 1])
            yield
            _act(p, RST.t[0:m, :], SSQ.t[0:m, :], AF.Ln, [SSQb], [RSTb], scale=1.0 / HV, bias=float(EPS))
            _act(p, RST.t[0:m, :], RST.t[0:m, :], AF.Exp, [RSTb], [RSTb], scale=-0.5)
            yield
            for h in range(NH):
                _stt(p, "dve", og.t[0:m, h * HV:(h + 1) * HV], bo[h][1], RST.t[0:m, h:h + 1],
                     RA.t[0:m, c, h * HV:(h + 1) * HV], ALU.mult, ALU.mult, [bo[h][0], RSTb, RAb], [OGb])
            yield
            bt = BK.get()
            btb = bt.t[:, :].bitcast(BF16)

            def trs(e):
                ins = None
                for kc in range(KC):
                    ins = e.transpose(btb[:, kc * m:(kc + 1) * m], og.t[0:m, kc * P:(kc + 1) * P], IDB[0:m, 0:m])
                return ins

            p.op("pe", trs, reads=[OGb, CB], writes=[bt])
            yield
            _cp(p, "act", dst_cols, btb[:, 0:KC * m].rearrange("p (k t) -> p k t", k=KC), [bt], dst_bufs)
            BK.put(bt)
            yield

        def run_zip(gens):
            gens = list(gens)
            while gens:
                for g in list(gens):
                    try:
                        next(g)
                    except StopIteration:
                        gens.remove(g)

        def gla_mix(l, tile, subs, hoist=None, skip0=False):
            i = l // 2
            scr.reset(SCR0)
            VA = scr.alloc("VA", [P, 9, 1024], BF16)
            RA = scr.alloc("RA", [P, 9, 1024], BF16)
            VAb = [Buf("va%d" % c) for c in range(9)]
            RAb = [Buf("ra%d" % c) for c in range(9)]
            GLT = scr.alloc("GLT", [P, NCOL], BF16)
            QKS = scr.alloc("QKS", [P, 8, NS], F32)
            OG = scr.alloc("OG", [P, 1024], BF16)
            SBF = scr.alloc("SBF", [P, 2, 1024], BF16)
            SBFb = [Buf("sbf0"), Buf("sbf1")]
            SSQ = scr.alloc("SSQ", [P, 4], F32)
            RST = scr.alloc("RST", [P, 4], F32)
            JNK = scr.alloc("JNK", [P, 256], BF16)
            SGT = scr.alloc("SGT", [P, 2, 512], BF16)
            SGTb = [Buf("sgt0"), Buf("sgt1")]
            tmpb = (SSQ, RST, JNK, OG)
            off_chunk = scr.off
            QK = scr.alloc("QK", [P, 8, NTOK], BF16)
            LAS = scr.alloc("LAS", [P, 512], F32)
            LH = scr.alloc("LH", [P, 512], BF16)
            LL = scr.alloc("LL", [P, 512], BF16)
            EQ = scr.alloc("EQ", [P, 512], F32)
            EK = scr.alloc("EK", [P, 512], F32)
            KS = scr.alloc("KS", [P, 512], BF16)
            KHT = scr.alloc("KHT", [P, 512], BF16)
            QS2 = [scr.alloc("QS%d" % k, [P, 512], BF16) for k in range(2)]
            KH2 = [scr.alloc("KH%d" % k, [P, 512], BF16) for k in range(2)]
            AT2 = [scr.alloc("AT%d" % k, [P, 512], BF16) for k in range(2)]
            EQL2 = [scr.alloc("EQL%d" % k, [P, 4], F32) for k in range(2)]
            extra = []

            norm_mod(l, 0, subs[1:] if skip0 else subs, dst_H)
            _cp(p, "act", SBF.t[:, 0, :], SST.t[:, i, :, :].rearrange("p h v -> p (h v)"), [SSTB[i]], [SBFb[0]])
            p.op("dve", lambda e: e.memset(GLT.t[0:17, :], 1.0), writes=[GLT])

            base, _ = widx[("qk", l)]
            rs = [ST.next(base + pc) for pc in range(N_QK)]
            for psubs in [subs[0:1], subs[1:]]:
                for pc in range(N_QK):
                    r = rs[pc]
                    for o2 in range(2):
                        oc = pc * 2 + o2
                        sc_ = float(HK ** -0.5) if oc < 4 else 1.0
                        for (st, o, w) in psubs:
                            bk = BK.get()
                            _mm(p, bk.t[:, 0:w], [(blk(r, o2 * 8 + kc), HT.t[:, kc, o:o + w]) for kc in range(KC)],
                                [r] + [HB[k][st] for k in range(KC)], [bk])
                            if st < 2:
                                _act(p, QK.t[:, oc, o:o + w], bk.t[:, 0:w], AF.Copy, [bk], [QK], scale=sc_)
                            else:
                                _act(p, QKS.t[:, oc, :], bk.t[:, 0:NS], AF.Copy, [bk], [QKS], scale=sc_)
                            BK.put(bk)
            for (st, o, w) in subs:
                bk = BK.get()
                _mm(p, bk.t[0:GR, 0:w], [(WGL.t[:, i, kc * GR:(kc + 1) * GR], HT.t[:, kc, o:o + w]) for kc in range(KC)],
                    [WGL] + [HB[k][st] for k in range(KC)], [bk])
                _cp(p, "dve", GLT.t[0:GR, o:o + w], bk.t[0:GR, 0:w], [bk], [GLT])
                BK.put(bk)
            base, _ = widx[("vr", l)]
            chunks = list(range(8)) + ([8] if tile == 1 else [])
            q = 0
            for grp in range(4):
                r0 = ST.next(base + 2 * grp)
                r1 = ST.next(base + 2 * grp + 1)
                for c in chunks:
                    if c < 8:
                        m, c0, c1, sti = P, c * P, (c + 1) * P, c // 4
                    else:
                        m, c0, c1, sti = NS, NTOK, NCOL, 2
                    bk = BK.get()
                    _mm(p, bk.t[0:m, :],
                        [(HT.t[:, kc, c0:c1], (r0 if kc < 4 else r1).t[:, (kc % 4) * 512:(kc % 4) * 512 + 512])
                         for kc in range(KC)], [r0, r1] + [HB[k][sti] for k in range(KC)], [bk])
                    if grp < 2:
                        _cp(p, "dve", VA.t[0:m, c, grp * 512:(grp + 1) * 512], bk.t[0:m, :], [bk], [VAb[c]])
                    else:
                        g2 = grp - 2
                        _act(p, SGT.t[0:m, q % 2, :], bk.t[0:m, :], AF.Silu, [bk], [SGTb[q % 2]])
                        gnb = GNR.t[0:m, i * HV:(i + 1) * HV].unsqueeze(1).broadcast_to([m, 2, HV])
                        _tt(p, "dve", RA.t[0:m, c, g2 * 512:(g2 + 1) * 512].rearrange("p (h v) -> p h v", h=2),
                            SGT.t[0:m, q % 2, :].rearrange("p (h v) -> p h v", h=2), gnb, ALU.mult,
                            [SGTb[q % 2], GNR], [RAb[c]])
                        q += 1
                    BK.put(bk)

            Uh = U.unsqueeze(1).broadcast_to([P, NH, P])
            v4 = lambda b: b.t[:, :].rearrange("p (h t) -> p h t", h=NH)

            def prep(c):
                o = c * P
                sl = c % 2
                QS, KH, AT, EQL = QS2[sl], KH2[sl], AT2[sl], EQL2[sl]
                bk1 = BK.get()
                _mm(p, bk1.t[:, :], [(GLT.t[0:17, o:o + P], W2A.t[0:17, i, :])], [GLT, W2A], [bk1])
                yield
                _act(p, LAS.t[:, :], bk1.t[:, :], AF.Exp, [bk1], [LAS], scale=-1.0)
                _act(p, LAS.t[:, :], LAS.t[:, :], AF.Ln, [LAS], [LAS], bias=1.0)
                BK.put(bk1)
                yield
                _cp(p, "dve", LH.t[:, :], LAS.t[:, :], [LAS], [LH])
                _tt(p, "dve", LL.t[:, :], LAS.t[:, :], LH.t[:, :], ALU.subtract, [LAS, LH], [LL])
                yield
                bk2 = BK.get()

                def cums(e, bk2=bk2):
                    ins = None
                    for h in range(NH):
                        e.matmul(bk2.t[:, h * P:(h + 1) * P], LH.t[:, h * P:(h + 1) * P], UB, start=True, stop=False)
                        ins = e.matmul(bk2.t[:, h * P:(h + 1) * P], LL.t[:, h * P:(h + 1) * P], UB, start=False, stop=True)
                    return ins

                p.op("pe", cums, reads=[LH, LL, CB], writes=[bk2])
                yield
                _act(p, EQ.t[:, :], bk2.t[:, :], AF.Exp, [bk2], [EQ], scale=-1.0 / 16)
                _act(p, EK.t[:, :], bk2.t[:, :], AF.Exp, [bk2], [EK], scale=1.0 / 16)
                BK.put(bk2)
                yield
                _cp(p, "dve", EQL.t[:, :], EQ.t[:, P - 1:512:P], [EQ], [EQL])
                _tt(p, "dve", v4(QS), QK.t[:, 0:4, o:o + P], v4(EQ), ALU.mult, [QK, EQ], [QS])
                _tt(p, "dve", v4(KS), QK.t[:, 4:8, o:o + P], v4(EK), ALU.mult, [QK, EK], [KS])
                yield
                for h in range(NH):
                    _stt(p, "dve", KHT.t[:, h * P:(h + 1) * P], QK.t[:, 4 + h, o:o + P], EQL.t[:, h:h + 1],
                         EK.t[:, h * P:(h + 1) * P], ALU.mult, ALU.mult, [QK, EQL, EK], [KHT])
                yield
                bk4 = BK.get()

                def amm(e, bk4=bk4, QS=QS):
                    ins = None
                    for h in range(NH):
                        ins = e.matmul(bk4.t[:, h * P:(h + 1) * P], KS.t[:, h * P:(h + 1) * P], QS.t[:, h * P:(h + 1) * P],
                                       start=True, stop=True)
                    return ins

                p.op("pe", amm, reads=[KS, QS], writes=[bk4])
                yield
                _tt(p, "dve", v4(AT), v4(bk4), Uh, ALU.mult, [bk4, CF], [AT])
                BK.put(bk4)
                yield
                bk3 = BK.get()
                b3 = bk3.t[:, :].bitcast(BF16)

                def trk(e, b3=b3):
                    ins = None
                    for h in range(NH):
                        ins = e.transpose(b3[:, h * P:(h + 1) * P], KHT.t[:, h * P:(h + 1) * P], IDB)
                    return ins

                p.op("pe", trk, reads=[KHT, CB], writes=[bk3])
                yield
                _cp(p, "act", KH.t[:, :], b3[:, 0:512], [bk3], [KH])
                BK.put(bk3)
                yield

            def scan(c):
                o = c * P
                sti = c // 4
                sl = c % 2
                QS, KH, AT, EQL = QS2[sl], KH2[sl], AT2[sl], EQL2[sl]
                bo = [BK.get(), BK.get()]
                cb = c % 2

                def omm(e, bo=bo, c=c, cb=cb, AT=AT, QS=QS):
                    ins = None
                    for h in range(NH):
                        oa = bo[h // 2].t[:, (h % 2) * HV:(h % 2 + 1) * HV]
                        e.matmul(oa, AT.t[:, h * P:(h + 1) * P], VA.t[:, c, h * HV:(h + 1) * HV], start=True, stop=False)
                        ins = e.matmul(oa, QS.t[:, h * P:(h + 1) * P], SBF.t[:, cb, h * HV:(h + 1) * HV], start=False, stop=True)
                    return ins

                p.op("pe", omm, reads=[AT, QS, VAb[c], SBFb[cb]], writes=bo)
                yield
                bs = [BK.get(), BK.get()]

                def smm(e, bs=bs, c=c, KH=KH):
                    ins = None
                    for h in range(NH):
                        sa = bs[h // 2].t[:, (h % 2) * HV:(h % 2 + 1) * HV]
                        ins = e.matmul(sa, KH.t[:, h * P:(h + 1) * P], VA.t[:, c, h * HV:(h + 1) * HV], start=True, stop=True)
                    return ins

                p.op("pe", smm, reads=[KH, VAb[c]], writes=bs)
                yield
                for h in range(NH):
                    _stt(p, "dve", SST.t[:, i, h, :], SST.t[:, i, h, :], EQL.t[:, h:h + 1],
                         bs[h // 2].t[:, (h % 2) * HV:(h % 2 + 1) * HV], ALU.mult, ALU.add,
                         [SSTB[i], EQL, bs[h // 2]], [SSTB[i]])
                BK.put(bs[0])
                BK.put(bs[1])
                yield
                _cp(p, "act", SBF.t[:, 1 - cb, :], SST.t[:, i, :, :].rearrange("p h v -> p (h v)"), [SSTB[i]], [SBFb[1 - cb]])
                yield
                bol = [(bo[h // 2], bo[h // 2].t[:, (h % 2) * HV:(h % 2 + 1) * HV]) for h in range(NH)]
                for _ in o_post(bol, P, c, OG, RA, RAb[c], SSQ, RST, JNK, tmpb, HT.t[:, :, o:o + P], [HB[k][sti] for k in range(KC)]):
                    yield
                BK.put(bo[0])
                BK.put(bo[1])
                yield

            run_zip([prep(0)])
            for c in range(8):
                gp_ = prep(c + 1) if c < 7 else None
                if gp_ is not None:
                    for _ in range(PREP_LEAD):
                        next(gp_, None)
                run_zip(([gp_] if gp_ is not None else []) + [scan(c)])

            if tile == 1:
                p.barrier()
                scr.reset(off_chunk)
                SIN4 = [scr.alloc("SIN%d" % k, [P, NH, HV], F32) for k in range(4)]
                QM = scr.alloc("QM", [P, NH, NS, NS], BF16)
                SINB2 = [scr.alloc("SINB%d" % k, [P, NH, HV], BF16) for k in range(2)]
                AS = scr.alloc("AS", [P, NH * NS], F32)
                bkx = BK.get()

                def xmm(e, bkx=bkx):
                    ins = None
                    for h in range(NH):
                        ins = e.matmul(bkx.t[:, h * NS:(h + 1) * NS], W2A.t[0:17, i, h * P:(h + 1) * P], GLT.t[0:17, NTOK:NCOL],
                                       start=True, stop=True)
                    return ins

                p.op("pe", xmm, reads=[W2A, GLT], writes=[bkx])
                _act(p, AS.t[:, :], bkx.t[:, 0:NH * NS], AF.Exp, [bkx], [AS], scale=-1.0)
                _act(p, AS.t[:, :], AS.t[:, :], AF.Ln, [AS], [AS], bias=1.0)
                _act(p, AS.t[:, :], AS.t[:, :], AF.Exp, [AS], [AS], scale=-1.0 / 16)
                BK.put(bkx)
                _tt(p, "dve", QM.t[:, :, :, :], QKS.t[:, 0:4, :].unsqueeze(2).broadcast_to([P, NH, NS, NS]),
                    I16B.unsqueeze(1).broadcast_to([P, NH, NS, NS]), ALU.mult, [QKS, CF], [QM])
                bos = [BK.get() for _ in range(NH)]
                def load_state(s):
                    SINs = SIN4[s % 4]
                    p.dma("sp", SINs.t[:, :, :], sg[i, s].rearrange("h d v -> d h v"), writes=[SINs])

                for s in range(4):
                    load_state(s)
                for s in range(NS):
                    SIN = SIN4[s % 4]
                    bv = [BK.get(), BK.get()]

                    def vmm(e, bv=bv, s=s):
                        ins = None
                        for hf in range(2):
                            ins = e.matmul(bv[hf].t[:, :], SEL.t[0:NS, s * P:(s + 1) * P], VA.t[0:NS, 8, hf * 512:(hf + 1) * 512],
                                           start=True, stop=True)
                        return ins

                    p.op("pe", vmm, reads=[SEL, VAb[8]], writes=bv)
                    for h in range(NH):
                        _act(p, SIN.t[:, h, :], SIN.t[:, h, :], AF.Copy, [SIN, AS], [SIN], scale=AS.t[:, h * NS + s:h * NS + s + 1])
                    for h in range(NH):
                        _stt(p, "dve", SIN.t[:, h, :], bv[h // 2].t[:, (h % 2) * HV:(h % 2 + 1) * HV],
                             QKS.t[:, 4 + h, s:s + 1], SIN.t[:, h, :], ALU.mult, ALU.add, [bv[h // 2], QKS, SIN], [SIN])
                    BK.put(bv[0])
                    BK.put(bv[1])
                    extra.append(p.dma("pool", gs[i, s].rearrange("h d v -> d h v"), SIN.t[:, :, :], reads=[SIN], out=True))

                    SINB = SINB2[s % 2]
                    _cp(p, "act", SINB.t[:, :, :], SIN.t[:, :, :], [SIN], [SINB])

                    def qmm(e, bos=bos, s=s, SINB=SINB):
                        ins = None
                        for h in range(NH):
                            ins = e.matmul(bos[h].t[0:NS, 0:HV], QM.t[:, h, s, :], SINB.t[:, h, :],
                                           start=(s == 0), stop=(s == NS - 1))
                        return ins

                    p.op("pe", qmm, reads=[QM, SINB], writes=bos)
                    if s + 4 < NS:
                        load_state(s + 4)
                bol = [(bos[h], bos[h].t[0:NS, 0:HV]) for h in range(NH)]
                for _ in o_post(bol, NS, 8, OG, RA, RAb[8], SSQ, RST, JNK, tmpb, HT.t[:, :, NTOK:NCOL], [HB[k][2] for k in range(KC)]):
                    pass
                for h in range(NH):
                    BK.put(bos[h])

            base, _ = widx[("wo", l)]
            for pi_, psubs in enumerate([subs[0:1], subs[1:]]):
                for pc in range(N_WO):
                    if pi_ == 1 and pc == 1 and hoist is not None:
                        hoist()
                    r = ST.next(base + pc)
                    for o2 in range(2):
                        oc = pc * 2 + o2
                        for (st, o, w) in psubs:
                            bk = BK.get()
                            _mm(p, bk.t[:, 0:w], [(blk(r, o2 * 8 + kc), HT.t[:, kc, o:o + w]) for kc in range(KC)],
                                [r] + [HB[k][st] for k in range(KC)], [bk])
                            resid_add(l, 2, oc, st, o, w, bk)
                            BK.put(bk)
            if tile == 1:
                extra.append(p.dma("sp", gp[i].rearrange("h d v -> d h v"), SST.t[:, i, :, :], reads=[SSTB[i]], out=True))
            p.phase_end(extra)

        YT_OFF = SCRW - KC * 512
        fin_extra = []

        YT = Buf("YT", SCRT.t[:, YT_OFF:SCRW].rearrange("p (k t) -> p k t", k=KC))
        YTb = [Buf("yt%d" % k) for k in range(KC)]

        def final_out(tile, subs):
            for (st, o, w) in subs:
                norm_stats(st, o, w)
                if st < 2:
                    for kc in range(KC):
                        y = YT.t[:, kc, 0:w]
                        _stt(p, "dve", y, XT.t[:, kc, o:o + w], PV.t[:, kc, 34:35], RSTD.t[:, 0:w], ALU.mult, ALU.mult,
                             [XB[kc][st], PV, RSTD], [YTb[kc]])
                        fin_extra.append(p.dma("sp", ypT[kc * P:(kc + 1) * P, tile * NTOK + o:tile * NTOK + o + w], y,
                                               reads=[YTb[kc]], out=True))
                else:
                    y = YT.t[:, 0, 0:KC * NS].rearrange("p (k s) -> p k s", k=KC)
                    rb = RSTD.t[:, 0:NS].unsqueeze(1).broadcast_to([P, KC, NS])
                    _tt(p, "dve", y, XT.t[:, :, o:o + NS], rb, ALU.mult, [XB[k][2] for k in range(KC)] + [RSTD], [YTb[0]])
                    _tt(p, "dve", y, y, PV.t[:, :, 34:35].broadcast_to([P, KC, NS]), ALU.mult, [YTb[0], PV], [YTb[0]])
                    fin_extra.append(p.dma("sp", ysT.rearrange("(k p) s -> p k s", p=P), y, reads=[YTb[0]], out=True))

        p.op("dve", lambda e: e.memset(SST.t[:, :, :, :], 0.0), writes=SSTB)
        stage = [0]

        def go():
            stage[0] += 1
            return _DBG_STOP is None or stage[0] <= _DBG_STOP

        for tile in range(2):
            subs = [(0, 0, 512), (1, 512, 512)] + ([(2, NTOK, NS)] if tile == 1 else [])
            if not go():
                break
            for sx in range(2):
                for kc in range(KC):
                    p.dma("sp", XT.t[:, kc, sx * 512:(sx + 1) * 512],
                          xpT[kc * P:(kc + 1) * P, tile * NTOK + sx * 512:tile * NTOK + (sx + 1) * 512], writes=[XB[kc][sx]])
            if tile == 1:
                p.dma("sp", XT.t[:, :, NTOK:NCOL], xsT.rearrange("(k p) s -> p k s", p=P), writes=[XB[k][2] for k in range(KC)])
            gla_hoisted = False
            for l in range(DEPTH):
                def h_ffn(l=l, subs=subs):
                    norm_mod(l, 3, subs[0:1], dst_H)

                if l % 2 == 0:
                    pool_mix(l, tile, subs, hoist=h_ffn, side=(ada_half(0, 1) if (tile == 0 and l == 0) else None))
                else:
                    gla_mix(l, tile, subs, hoist=h_ffn, skip0=gla_hoisted)
                nxt_gla = (l + 1 < DEPTH and (l + 1) % 2 == 1)

                def h_gla(l=l, subs=subs):
                    norm_mod(l + 1, 0, subs[0:1], dst_H)

                def h_fin(tile=tile, subs=subs):
                    final_out(tile, subs[0:1])

                ffn(l, subs, side=(ada_layer(l + 1) if (tile == 0 and l + 1 < DEPTH) else None),
                    hoist=(h_gla if nxt_gla else (h_fin if l == DEPTH - 1 else None)), skip0=True)
                gla_hoisted = nxt_gla
            final_out(tile, subs[1:])
            p.phase_end(fin_extra)
            del fin_extra[:]
        p.finish()
    return nc


_NC_CACHE = {}
_PREP_ONLY = False
_DBG_STOP = None
_DBG_GLA = None


class _Abort(Exception):
    pass


def _gchk(n):
    if _DBG_GLA is not None and n >= _DBG_GLA:
        raise _Abort()


def _consts():
    s = np.arange(P)
    U = (s[:, None] <= s[None, :]).astype(np.float32)
    I16 = np.broadcast_to(np.eye(NS, dtype=np.float32)[None], (P, NS, NS)).reshape(P, NS * NS)
    invc = np.zeros((4, 16), np.float32)
    for g, w in enumerate(WIN):
        invc[g] = 1.0 / np.minimum(np.arange(16) + 1, w)
    INVC = np.broadcast_to(invc.reshape(1, 64), (P, 64))
    cst_f = np.ascontiguousarray(np.concatenate([U, I16, INVC], axis=1), dtype=np.float32)
    cst_b = np.ascontiguousarray(np.concatenate([np.ones((P, P), np.float32), np.eye(P, dtype=np.float32), U], axis=1))
    sel = np.zeros((NS, NS, P), np.float32)
    for i in range(NS):
        sel[i, i, :] = 1.0
    return cst_f, cst_b, sel.reshape(NS, NS * P)


def kernel(x_prompt, x_sample, c_prompt, c_sample, state_pool, state_gla, g_mix, g_ffn,
           w_ada, b_ada, w_pool, s_pool, w_gla_in, w_gla_g2, b_gla_g, g_gla_norm, w_gla_o,
           w_ffn_gate, w_ffn_up, w_ffn_down, g_final):
    f = lambda a: np.asarray(a, dtype=np.float32)
    x_prompt, x_sample, c_prompt, c_sample = f(x_prompt), f(x_sample), f(c_prompt), f(c_sample)
    state_pool, state_gla = f(state_pool), f(state_gla)
    g_mix, g_ffn, w_ada, b_ada, w_pool, s_pool = f(g_mix), f(g_ffn), f(w_ada), f(b_ada), f(w_pool), f(s_pool)
    w_gla_in, w_gla_g2, b_gla_g, g_gla_norm, w_gla_o = f(w_gla_in), f(w_gla_g2), f(b_gla_g), f(g_gla_norm), f(w_gla_o)
    w_ffn_gate, w_ffn_up, w_ffn_down, g_final = f(w_ffn_gate), f(w_ffn_up), f(w_ffn_down), f(g_final)
    NCORES = 8

    wst, _ = build_wstream(w_ada, w_pool, w_gla_in, w_gla_o, w_ffn_gate, w_ffn_up, w_ffn_down)
    pvec = np.zeros((D, NV), np.float32)
    pvec[:, 0:4] = g_mix.T
    pvec[:, 4:8] = g_ffn.T
    for l in range(DEPTH):
        pvec[:, 8 + l * 6:14 + l * 6] = b_ada[l].reshape(6, D).T
    pvec[:, 32:34] = s_pool.T
    pvec[:, 34] = g_final
    w2a = np.concatenate([w_gla_g2, b_gla_g[:, None, :]], axis=1)
    wgl = np.ascontiguousarray(w_gla_in[:, :, 3072:3088].reshape(2, KC, P, GR).transpose(0, 2, 1, 3)).reshape(2, P, KC * GR)
    gnr = np.ascontiguousarray(np.broadcast_to(g_gla_norm.reshape(1, 2 * HV), (P, 2 * HV)))
    cst_f, cst_b, sel = _consts()

    in_maps = []
    for b in range(NCORES):
        ss = slice(b * NS, (b + 1) * NS)
        cT = np.concatenate([c_prompt[b][None, :], c_sample[ss]], axis=0).T
        spT = state_pool[:, ss].transpose(0, 3, 1, 2).reshape(2, D, NS * 15)
        in_maps.append({
            "xpT": np.ascontiguousarray(x_prompt[b].T),
            "xsT": np.ascontiguousarray(x_sample[ss, 0, :].T),
            "cT": np.ascontiguousarray(cT),
            "spT": np.ascontiguousarray(spT),
            "sg": np.ascontiguousarray(state_gla[:, ss]),
            "wst": wst,
            "pvec": pvec,
            "w2a": np.ascontiguousarray(w2a),
            "wgl": wgl,
            "gnr": gnr,
            "cst_f": cst_f,
            "cst_b": cst_b,
            "sel": sel,
        })
    if _PREP_ONLY:
        return in_maps
    if "nc" not in _NC_CACHE:
        _NC_CACHE["nc"] = build_program()
    nc = _NC_CACHE["nc"]
    res = run_bass_kernel_spmd(nc, in_maps, core_ids=list(range(NCORES)))
    R = res.results
    y_prompt = np.stack([R[b]["ypT"].T for b in range(NCORES)], axis=0)
    y_sample = np.concatenate([R[b]["ysT"].T for b in range(NCORES)], axis=0)[:, None, :]
    pool_p = np.stack([R[b]["ppT"].transpose(0, 2, 1) for b in range(NCORES)], axis=1)
    pool_s = np.concatenate([R[b]["psT"].reshape(2, D, NS, 15).transpose(0, 2, 3, 1) for b in range(NCORES)], axis=1)
    gla_p = np.stack([R[b]["gp"] for b in range(NCORES)], axis=1)
    gla_s = np.concatenate([R[b]["gs"] for b in range(NCORES)], axis=1)
    c = lambda a: np.ascontiguousarray(a, dtype=np.float32)
    return (c(y_prompt), c(y_sample), c(pool_p), c(pool_s), c(gla_p), c(gla_s))
```
